# Optimizing a Trainium2 kernel written in Bass

```python
import jax, jax.numpy as jnp
from jax import lax
import numpy as np

D_MODEL = 1024
BATCH = 8
SEQ = 4096
DEPTH = 2

HEAD_DIM = 64
N_HEADS_A = 6
N_HEADS_C = 6
WIDTH_A = N_HEADS_A * HEAD_DIM
WIDTH_C = N_HEADS_C * HEAD_DIM
WIDTH_B = D_MODEL - WIDTH_A - WIDTH_C
POOL_WINDOWS = (2, 4, 8, 16)
N_POOL_GROUPS = len(POOL_WINDOWS)
POOL_GROUP_DIM = WIDTH_B // N_POOL_GROUPS
DILATED_CONFIGS = ((128, 1), (512, 4), (2048, 16))
GRID_W = 64
NA_ROWS_MAX = 8
NA_COLS = 16
D_FF = -(-(8 * D_MODEL) // (3 * 256)) * 256
PROJ_WIDTH = 3 * WIDTH_A + WIDTH_B + 3 * WIDTH_C
EPS = 1e-6
NEG = -1e30

kernel_name = "hybrid_dilated_pool_neighbourhood_encoder"


def rmsnorm(x, g):
    xf = x.astype(jnp.float32)
    y = xf * lax.rsqrt(jnp.mean(xf * xf, axis=-1, keepdims=True) + EPS)
    return (y * g.astype(jnp.float32)).astype(x.dtype)


def to_heads(a, n_heads):
    b, s, _ = a.shape
    return a.reshape(b, s, n_heads, HEAD_DIM).transpose(0, 2, 1, 3)


def from_heads(a):
    b, h, s, d = a.shape
    return a.transpose(0, 2, 1, 3).reshape(b, s, h * d)


def dilated_window_branch(q, k, v, slopes, window, dil):
    B, H, S, hd = q.shape
    half = window // (2 * dil)
    blk = half
    chunk = dil * blk
    L = -(-S // chunk) * chunk
    n = L // dil
    nb = n // blk

    def prep(a):
        a = jnp.pad(a, ((0, 0), (0, 0), (0, L - S), (0, 0)))
        a = a.reshape(B, H, n, dil, hd).transpose(0, 1, 3, 2, 4)
        return a.reshape(B, H, dil, nb, blk, hd)

    def band(a):
        z = jnp.zeros_like(a[:, :, :, :1])
        prev = jnp.concatenate([z, a[:, :, :, :-1]], axis=3)
        nxt = jnp.concatenate([a[:, :, :, 1:], z], axis=3)
        return jnp.concatenate([prev, a, nxt], axis=4)

    qb = prep(q)
    kn = band(prep(k))
    vn = band(prep(v))

    qi = jnp.arange(nb)[:, None] * blk + jnp.arange(blk)[None, :]
    ki = (jnp.arange(nb)[:, None] - 1) * blk + jnp.arange(3 * blk)[None, :]
    rel = ki[:, None, :] - qi[:, :, None]
    kpos = ki[None] * dil + jnp.arange(dil)[:, None, None]
    kvalid = (ki >= 0)[None] & (kpos < S)
    allowed = (jnp.abs(rel) <= half)[None] & kvalid[:, :, None, :]
    dist = (jnp.abs(rel) * dil).astype(jnp.float32)

    s = jnp.einsum('bhrnqd,bhrnkd->bhrnqk', qb, kn,
                   preferred_element_type=jnp.float32) * (hd ** -0.5)
    s = s - slopes[None, :, None, None, None, None] * dist[None, None, None]
    s = jnp.where(allowed[None, None], s, NEG)
    m = jnp.max(s, axis=-1, keepdims=True)
    p = jnp.exp(s - m)
    den = jnp.sum(p, axis=-1)
    o = jnp.einsum('bhrnqk,bhrnkd->bhrnqd', p.astype(v.dtype), vn,
                   preferred_element_type=jnp.float32) / den[..., None]
    lse = m[..., 0] + jnp.log(den)
    o = o.reshape(B, H, dil, n, hd).transpose(0, 1, 3, 2, 4).reshape(B, H, L, hd)[:, :, :S]
    lse = lse.reshape(B, H, dil, n).transpose(0, 1, 3, 2).reshape(B, H, L)[:, :, :S]
    return o, lse


def dilated_mixture_attention(q, k, v):
    n_h = q.shape[1]
    slopes = 2.0 ** (-8.0 * (jnp.arange(n_h, dtype=jnp.float32) + 1.0) / n_h)
    outs, lses = [], []
    for window, dil in DILATED_CONFIGS:
        o, lse = dilated_window_branch(q, k, v, slopes, window, dil)
        outs.append(o)
        lses.append(lse)
    w = jax.nn.softmax(jnp.stack(lses, axis=0), axis=0)
    return jnp.sum(w[..., None] * jnp.stack(outs, axis=0), axis=0)


def neighbourhood_attention(q, k, v, rpb):
    B, H, S, hd = q.shape
    R = S // GRID_W
    wr = min(NA_ROWS_MAX, R)
    q = q.reshape(B, H, R, GRID_W, hd)
    k = k.reshape(B, H, R, GRID_W, hd)
    v = v.reshape(B, H, R, GRID_W, hd)
    rows = jnp.arange(R)
    rstart = jnp.clip(rows - wr // 2, 0, R - wr)
    krow = rstart[:, None] + jnp.arange(wr)[None, :]
    kg = k[:, :, krow]
    vg = v[:, :, krow]
    cols = jnp.arange(GRID_W)
    cstart = jnp.clip(cols - NA_COLS // 2, 0, GRID_W - NA_COLS)
    col_in = (cols[None, :] >= cstart[:, None]) & (cols[None, :] < cstart[:, None] + NA_COLS)
    s = jnp.einsum('bhrqd,bhrikd->bhrqik', q, kg,
                   preferred_element_type=jnp.float32) * (hd ** -0.5)
    dr = krow - rows[:, None] + (NA_ROWS_MAX - 1)
    dc = jnp.clip(cols[None, :] - cols[:, None] + NA_COLS - 1, 0, 2 * NA_COLS - 2)
    bias = rpb[:, dr[:, None, :, None], dc[None, :, None, :]]
    s = s + bias[None].astype(jnp.float32)
    s = jnp.where(col_in[:, None, :], s, NEG)
    p = jax.nn.softmax(s, axis=(-2, -1))
    o = jnp.einsum('bhrqik,bhrikd->bhrqd', p.astype(v.dtype), vg,
                   preferred_element_type=jnp.float32)
    return o.reshape(B, H, S, hd)


def multiscale_pool(u, w_pool, pool_scale):
    B, S, _ = u.shape
    uf = u.astype(jnp.float32)
    csum = jnp.concatenate([jnp.zeros((B, 1, WIDTH_B), jnp.float32), jnp.cumsum(uf, axis=1)], axis=1)
    t = jnp.arange(S)
    outs = []
    for g, w in enumerate(POOL_WINDOWS):
        lo = jnp.clip(t - w // 2, 0, S - 1)
        hi = jnp.clip(t + w // 2 - 1, 0, S - 1)
        seg = csum[:, :, g * POOL_GROUP_DIM:(g + 1) * POOL_GROUP_DIM]
        tot = seg[:, hi + 1] - seg[:, lo]
        cnt = (hi - lo + 1).astype(jnp.float32)
        outs.append(tot / cnt[None, :, None])
    pooled = jnp.concatenate(outs, axis=-1) - uf
    pooled = pooled.reshape(B, S, N_POOL_GROUPS, POOL_GROUP_DIM)
    y = jnp.einsum('bsgc,gcd->bsgd', pooled, w_pool.astype(jnp.float32)).reshape(B, S, WIDTH_B)
    return y * pool_scale.astype(jnp.float32)


def setup_inputs(seed: int = 0) -> dict:
    key = jax.random.key(seed)
    ks = jax.random.split(key, 16)
    f32 = jnp.float32
    nrm = lambda k, shape, sc: jax.random.normal(k, shape, f32) * sc
    return {
        "x": nrm(ks[0], (BATCH, SEQ, D_MODEL), 1.0),
        "c": nrm(ks[1], (BATCH, D_MODEL), 1.0),
        "w_ada": nrm(ks[2], (DEPTH, D_MODEL, 6 * D_MODEL), D_MODEL ** -0.5),
        "b_ada": nrm(ks[3], (DEPTH, 6 * D_MODEL), 0.02),
        "norm_mix": 1.0 + nrm(ks[4], (DEPTH, D_MODEL), 0.05),
        "w_in": nrm(ks[5], (DEPTH, D_MODEL, PROJ_WIDTH), D_MODEL ** -0.5),
        "norm_a_out": 1.0 + nrm(ks[6], (DEPTH, WIDTH_A), 0.05),
        "norm_c_out": 1.0 + nrm(ks[7], (DEPTH, WIDTH_C), 0.05),
        "w_pool": nrm(ks[8], (DEPTH, N_POOL_GROUPS, POOL_GROUP_DIM, POOL_GROUP_DIM), POOL_GROUP_DIM ** -0.5),
        "pool_scale": 1.0 + nrm(ks[9], (DEPTH, WIDTH_B), 0.1),
        "rpb": nrm(ks[10], (DEPTH, N_HEADS_C, 2 * NA_ROWS_MAX - 1, 2 * NA_COLS - 1), 0.1),
        "w_out": nrm(ks[11], (DEPTH, D_MODEL, D_MODEL), D_MODEL ** -0.5),
        "norm_ffn": 1.0 + nrm(ks[12], (DEPTH, D_MODEL), 0.05),
        "w_ffn_in": nrm(ks[13], (DEPTH, D_MODEL, 2 * D_FF), D_MODEL ** -0.5),
        "w_ffn_out": nrm(ks[14], (DEPTH, D_FF, D_MODEL), D_FF ** -0.5),
        "norm_final": 1.0 + nrm(ks[15], (D_MODEL,), 0.05),
    }


def reference(x, c, w_ada, b_ada, norm_mix, w_in, norm_a_out, norm_c_out, w_pool,
              pool_scale, rpb, w_out, norm_ffn, w_ffn_in, w_ffn_out, norm_final):
    dt = x.dtype
    c_act = jax.nn.silu(c)
    splits = np.cumsum([WIDTH_A, WIDTH_A, WIDTH_A, WIDTH_B, WIDTH_C, WIDTH_C]).tolist()
    for l in range(DEPTH):
        mod = c_act @ w_ada[l] + b_ada[l]
        sh1, sc1, g1, sh2, sc2, g2 = jnp.split(mod, 6, axis=-1)

        h = rmsnorm(x, norm_mix[l]) * (1.0 + sc1[:, None]) + sh1[:, None]
        z = h @ w_in[l]
        qa, ka, va, ub, qc, kc, vc = jnp.split(z, splits, axis=-1)

        oa = dilated_mixture_attention(to_heads(qa, N_HEADS_A), to_heads(ka, N_HEADS_A),
                                       to_heads(va, N_HEADS_A))
        oa = rmsnorm(from_heads(oa), norm_a_out[l]).astype(dt)
        ob = multiscale_pool(ub, w_pool[l], pool_scale[l]).astype(dt)
        oc = neighbourhood_attention(to_heads(qc, N_HEADS_C), to_heads(kc, N_HEADS_C),
                                     to_heads(vc, N_HEADS_C), rpb[l])
        oc = rmsnorm(from_heads(oc), norm_c_out[l]).astype(dt)

        mix = jnp.concatenate([oa, ob, oc], axis=-1) @ w_out[l]
        x = x + g1[:, None] * mix

        h2 = rmsnorm(x, norm_ffn[l]) * (1.0 + sc2[:, None]) + sh2[:, None]
        gate, up = jnp.split(h2 @ w_ffn_in[l], 2, axis=-1)
        ffn = (jax.nn.silu(gate) * up) @ w_ffn_out[l]
        x = x + g2[:, None] * ffn
    return rmsnorm(x, norm_final)
```

```python
import numpy as np
from contextlib import ExitStack
import concourse.bass as bass
import concourse.mybir as mybir
from concourse.bass_utils import run_bass_kernel_spmd

F32 = mybir.dt.float32
BF16 = mybir.dt.bfloat16
AF = mybir.ActivationFunctionType
ALU = mybir.AluOpType

S_ = 4096
D_ = 1024
DFF = 2816
NJ = DFF // 128
PW = 2560
EPS = 1e-6
NEGB = -30000.0
NCOL = 8 + 2 * 72


class Buf:
    __slots__ = ("name", "w", "r", "sem", "cnt")

    def __init__(self, name):
        self.name = name
        self.w = {}
        self.r = {}
        self.sem = None
        self.cnt = 0


class Eng:
    def __init__(self, eng, sem, name, same_wait=True):
        self.eng = eng
        self.sem = sem
        self.cnt = 0
        self.seen = {}
        self.name = name
        self.same_wait = same_wait


class TB:
    def __init__(self, t, name):
        self.t = t
        self.b = Buf(name)


class Sched:
    def __init__(self, nc, es):
        self.nc = nc
        self.es = es
        mk = lambda n: es.enter_context(nc.semaphore(n))
        self.pe = Eng(nc.tensor, mk("s_pe"), "pe", same_wait=False)
        self.act = Eng(nc.scalar, mk("s_act"), "act")
        self.dve = Eng(nc.vector, mk("s_dve"), "dve")
        self.pool = Eng(nc.gpsimd, mk("s_pool"), "pool")
        self.sp = Eng(nc.sync, mk("s_sp"), "sp")
        self.engs = [self.pe, self.act, self.dve, self.pool, self.sp]
        self.dma_bufs = []
        self.nsem = 5
        self.dead = False

    def _wait(self, E, deps):
        for (sem, val, own) in deps:
            if own is E and not E.same_wait:
                continue
            k = id(sem)
            if E.seen.get(k, 0) < val:
                E.eng.wait_ge(sem, val)
                E.seen[k] = val

    @staticmethod
    def _merge(dct, tok):
        k = id(tok[0])
        if k not in dct or dct[k][1] < tok[1]:
            dct[k] = tok

    def _deps(self, reads, writes, par):
        deps = []
        for b in reads:
            deps.extend(b.w.values())
        for b in writes:
            if not par:
                deps.extend(b.w.values())
            deps.extend(b.r.values())
        return deps

    def op(self, E, fn, reads=(), writes=(), par=False):
        if self.dead:
            return None
        self._wait(E, self._deps(reads, writes, par))
        inst = fn()
        E.cnt += 1
        inst.then_inc(E.sem, 1)
        tok = (E.sem, E.cnt, E)
        for b in reads:
            self._merge(b.r, tok)
        for b in writes:
            self._merge(b.w, tok)
        return inst

    def dma(self, out, in_, reads=(), writes=(), par=False, E=None, own=None, **kw):
        if self.dead:
            return
        E = E or self.sp
        self._wait(E, self._deps(reads, writes, par))
        wb = own or writes[0]
        if wb.sem is None:
            wb.sem = self.es.enter_context(self.nc.semaphore("d%d" % self.nsem))
            self.nsem += 1
            self.dma_bufs.append(wb)
        wb.cnt += 16
        E.eng.dma_start(out=out, in_=in_, **kw).then_inc(wb.sem, 16)
        tok = (wb.sem, wb.cnt, None)
        for b in reads:
            self._merge(b.r, tok)
        for b in writes:
            self._merge(b.w, tok)

    def _all(self):
        toks = [(E.sem, E.cnt, None) for E in self.engs if E.cnt > 0]
        toks += [(b.sem, b.cnt, None) for b in self.dma_bufs if b.cnt > 0]
        return toks

    def barrier(self):
        if self.dead:
            return
        toks = self._all()
        for E in self.engs:
            self._wait(E, toks)

    def finish(self):
        self._wait(self.sp, self._all())


def host_constants():
    j = np.arange(128)[:, None]
    c = np.arange(256)[None, :]
    rel = (c - 64) - j
    ok = np.abs(rel) <= 64
    slopes = 2.0 ** (-8.0 * (np.arange(6, dtype=np.float64) + 1.0) / 6.0)
    ba = np.empty((128, 3, 6, 256), np.float32)
    for bi, d in enumerate((1, 4, 16)):
        for h in range(6):
            v = -(slopes[h] * d) * np.abs(rel)
            ba[:, bi, h, :] = np.where(ok, np.exp(v), 0.0).astype(np.float32)
    cols = np.arange(64)
    cstart = np.clip(cols - 8, 0, 48)
    kc = np.arange(64)[:, None]
    inwin = (kc >= cstart[None, :]) & (kc < cstart[None, :] + 16)
    cm = np.where(inwin, 1.0, 0.0).astype(np.float32)
    cmask = np.ascontiguousarray(np.broadcast_to(np.concatenate([cm, cm], axis=0)[:, None, :], (128, 14, 64)))
    t = np.arange(S_)
    inv = np.empty((2, 128, S_), np.float32)
    for g, w in enumerate((2, 4, 8, 16)):
        lo = np.clip(t - w // 2, 0, S_ - 1)
        hi = np.clip(t + w // 2 - 1, 0, S_ - 1)
        inv[g // 2, (g % 2) * 64:(g % 2) * 64 + 64, :] = (1.0 / (hi - lo + 1).astype(np.float64)).astype(np.float32)[None, :]
    return ba, cmask, inv


class _Stop(Exception):
    pass


def build_nc(dbg=False, stop=None):
    nc = bass.Bass("TRN2", target_bir_lowering=False)
    dt_in = lambda n, s, d=F32: nc.dram_tensor(n, s, d, kind="ExternalInput").ap()
    x_in = dt_in("x", [S_, D_])
    colpack = dt_in("colpack", [128, NCOL])
    nfb_in = dt_in("nfb", [128, D_])
    w_ada = dt_in("w_ada", [2, D_, 6 * D_])
    w_in = dt_in("w_in", [2, D_, PW])
    w_pool = dt_in("w_pool", [2, 4, 64, 64])
    rpbt = dt_in("rpbt", [2, 6, 15, 64, 64])
    w_out = dt_in("w_out", [2, D_, D_])
    w_fi = dt_in("w_ffn_in", [2, D_, 2 * DFF])
    w_fo = dt_in("w_ffn_out", [2, DFF, D_])
    ba_in = dt_in("ba_tab", [128, 3, 6, 256])
    cm_in = dt_in("cmask", [128, 14, 64])
    inv_in = dt_in("invcnt", [2, 128, S_])
    y_out = nc.dram_tensor("y", [S_, D_], F32, kind="ExternalOutput").ap()
    kscr = "ExternalOutput" if dbg else "Internal"
    dscr = lambda n, s, d: nc.dram_tensor(n, s, d, kind=kscr).ap()
    qk_scr = dscr("qk_scr", [12, 128, S_], BF16)
    v_scr = dscr("v_scr", [2, S_, 384], BF16)
    u_scr = dscr("u_scr", [2, 128, S_], F32)
    xs1 = dscr("xs1", [S_, D_], F32)
    wi_scr = dscr("wi_scr", [2, NJ, 128, 2 * 8 * 128], BF16)
    wo_scr = dscr("wo_scr", [2, DFF, D_], BF16)
    wout_scr = dscr("wout_scr", [2, D_, D_], BF16)
    ec_scr = dscr("ec_scr", [2, 128, 6 * 14 * 64], BF16)
    mix_dbg = dscr("mix_dbg", [2, 8, 128, S_], BF16) if dbg else None

    with ExitStack() as es:
        S = Sched(nc, es)

        uid = [0]

        def sb(st, name, shape, dt):
            uid[0] += 1
            name = "%s_%d" % (name, uid[0])
            return TB(st.enter_context(nc.sbuf_tensor(name, shape, dt)), name)

        psall = es.enter_context(nc.psum_tensor("psall", [128, 8 * 512], F32))
        ps3 = psall[:, :].rearrange("p (b c) -> p b c", b=8)
        pb = [TB(psall[:, i * 512:(i + 1) * 512], "pb%d" % i) for i in range(8)]
        identf = sb(es, "identf", [128, 128], F32)
        onesf = sb(es, "onesf", [128, 128], F32)
        onesb = sb(es, "onesb", [128, 128], BF16)
        identb = sb(es, "identb", [128, 128], BF16)
        mixT = sb(es, "mixT", [128, 8, S_], BF16)
        cols = sb(es, "cols", [128, NCOL], F32)
        cact2 = sb(es, "cact2", [128, 8, 2], F32)
        modcol = sb(es, "modcol", [128, 48], F32)
        A1 = sb(es, "A1", [128, 8], F32)
        A2 = sb(es, "A2", [128, 8], F32)
        nfb = sb(es, "nfb_sb", [128, D_], F32)
        diag = [sb(es, "diag%d" % i, [128, 128], F32) for i in range(2)]
        ssq = sb(es, "ssq", [128, 4], F32)
        rs = sb(es, "rs", [128, 4], F32)
        junk = sb(es, "junk", [128, D_], BF16)
        qkb = [Buf("qk%d" % i) for i in range(12)]
        vb = [Buf("v%d" % i) for i in range(2)]
        ub = [Buf("u%d" % i) for i in range(2)]
        xs1b = Buf("xs1")
        yb = Buf("y")
        wib, wob, woutb = Buf("wi"), Buf("wo"), Buf("wout")
        ecb = Buf("ec")
        mixdbgb = Buf("mixdbg")

        S.op(S.pool, lambda: nc.gpsimd.memset(onesf.t[:, :], 1.0), writes=[onesf.b])
        S.op(S.pool, lambda: nc.gpsimd.memset(onesb.t[:, :], 1.0), writes=[onesb.b])
        S.op(S.pool, lambda: nc.gpsimd.affine_select(out=identf.t[:, :], in_=onesf.t[:, :], pattern=[[-1, 128]],
                                                     compare_op=ALU.is_equal, fill=0.0, base=0, channel_multiplier=1),
             reads=[onesf.b], writes=[identf.b])
        S.op(S.pool, lambda: nc.gpsimd.tensor_copy(out=identb.t[:, :], in_=identf.t[:, :]), reads=[identf.b], writes=[identb.b])
        S.dma(cols.t[:, :], colpack, writes=[cols.b])
        S.dma(nfb.t[:, :], nfb_in, writes=[nfb.b])
        for q in range(2):
            S.op(S.act, lambda: nc.scalar.activation(out=cact2.t[:, :, q], in_=cols.t[:, 0:8], func=AF.Silu),
                 reads=[cols.b], writes=[cact2.b], par=True)

        def rstd_from(ssq_ap, out_ap, n_feat, rb, wbuf):
            S.op(S.dve, lambda: nc.vector.tensor_scalar(out=out_ap, in0=ssq_ap, scalar1=1.0 / n_feat, scalar2=EPS,
                                                        op0=ALU.mult, op1=ALU.add), reads=rb, writes=[wbuf])
            S.op(S.act, lambda: nc.scalar.activation(out=out_ap, in_=out_ap, func=AF.Sqrt), reads=[wbuf], writes=[wbuf])
            S.op(S.dve, lambda: nc.vector.reciprocal(out=out_ap, in_=out_ap), reads=[wbuf], writes=[wbuf])

        evac_ctr = [0]

        def evac(out_ap, in_ap, reads, writes, scale=1.0, par=True):
            evac_ctr[0] += 1
            if evac_ctr[0] % 2 == 0:
                S.op(S.act, lambda: nc.scalar.activation(out=out_ap, in_=in_ap, func=AF.Copy, scale=float(scale)),
                     reads=reads, writes=writes, par=par)
            else:
                S.op(S.dve, lambda: nc.vector.tensor_scalar(out=out_ap, in0=in_ap, scalar1=float(scale), scalar2=None,
                                                            op0=ALU.mult), reads=reads, writes=writes, par=par)

        def ck(tag):
            if stop == tag:
                S.barrier()
                S.dead = True
        try:
          for l in range(2):
              cb = 8 + 72 * l
              x_src = x_in if l == 0 else xs1
              x_dst = xs1 if l == 0 else y_out
              xdb = xs1b if l == 0 else yb
              xsb = None if l == 0 else xs1b

              with ExitStack() as ls:
                  winb = sb(ls, "winb", [128, 8, PW], BF16)
                  g1b = sb(ls, "g1b", [128, D_], F32)
                  g2b = sb(ls, "g2b", [128, D_], F32)
                  with ExitStack() as ps_:
                      NST = 4
                      stage = [sb(ps_, "stage%d" % i, [128, 3 * D_], F32) for i in range(NST)]
                      obuf = [sb(ps_, "obuf%d" % i, [128, DFF], BF16) for i in range(NST)]
                      pmod = pb[6].t[:, 0:96].rearrange("p (j q) -> p j q", q=2)
                      modrow = sb(ps_, "modrow", [2, 3 * D_], F32)
                      ada_pieces = [(k, hf) for hf in range(2) for k in range(8)]

                      upieces = []
                      for i_ in range(len(ada_pieces)):
                          upieces.append(("ada",) + ada_pieces[i_])
                          if i_ % 2 == 1:
                              upieces.append(("win", i_ // 2))

                      def u_load(i):
                          pc_ = upieces[i]
                          stg = stage[i % NST]
                          if pc_[0] == "ada":
                              _, k, hf = pc_
                              S.dma(stg.t[:, :], w_ada[l, k * 128:(k + 1) * 128, hf * 3072:(hf + 1) * 3072], writes=[stg.b])
                          else:
                              S.dma(stg.t[:, 0:PW], w_in[l, pc_[1] * 128:(pc_[1] + 1) * 128, :], writes=[stg.b])
                      for i in range(3):
                          u_load(i)
                      for i, pc_ in enumerate(upieces):
                          stg = stage[i % NST]
                          if pc_[0] == "win":
                              kw_ = pc_[1]
                              S.op(S.act, lambda: nc.scalar.copy(out=winb.t[:, kw_, :], in_=stg.t[:, 0:PW]),
                                   reads=[stg.b], writes=[winb.b], par=True)
                          else:
                              _, k, hf = pc_

                              def f():
                                  for n_ in range(6):
                                      ins = nc.tensor.matmul(pb[n_].t[0:2, :], lhsT=cact2.t[:, k, :], rhs=stg.t[:, n_ * 512:(n_ + 1) * 512],
                                                             start=(k == 0), stop=(k == 7))
                                  return ins
                              S.op(S.pe, f, reads=[stg.b, cact2.b], writes=[pb[n_].b for n_ in range(6)])
                              if k == 7:
                                  for n_ in range(6):
                                      S.op(S.act, lambda: nc.scalar.copy(out=modrow.t[0:2, n_ * 512:(n_ + 1) * 512],
                                                                         in_=pb[n_].t[0:2, :]), reads=[pb[n_].b], writes=[modrow.b], par=(n_ > 0))

                                  def ftr():
                                      for jj in range(24):
                                          ins = nc.tensor.matmul(pmod[:, hf * 24 + jj, :], lhsT=modrow.t[0:1, jj * 128:(jj + 1) * 128],
                                                                 rhs=onesf.t[0:1, 0:2], start=True, stop=True, skip_group_check=True)
                                      return ins
                                  S.op(S.pe, ftr, reads=[modrow.b, onesf.b], writes=[pb[6].b])
                          if i + 3 < len(upieces):
                              u_load(i + 3)
                      S.op(S.dve, lambda: nc.vector.tensor_tensor(out=modcol.t[:, :], in0=pmod[:, :, 0],
                                                                  in1=cols.t[:, cb:cb + 48], op=ALU.add),
                           reads=[pb[6].b, cols.b], writes=[modcol.b])
                      S.op(S.dve, lambda: nc.vector.scalar_tensor_tensor(out=A1.t[:, :], in0=modcol.t[:, 8:16], scalar=1.0,
                                                                         in1=cols.t[:, cb + 48:cb + 56], op0=ALU.add, op1=ALU.mult),
                           reads=[modcol.b, cols.b], writes=[A1.b])
                      S.op(S.dve, lambda: nc.vector.scalar_tensor_tensor(out=A2.t[:, :], in0=modcol.t[:, 32:40], scalar=1.0,
                                                                         in1=cols.t[:, cb + 56:cb + 64], op0=ALU.add, op1=ALU.mult),
                           reads=[modcol.b, cols.b], writes=[A2.b])
                      for gi, (gb, gc0) in enumerate(((g1b, 16), (g2b, 40))):
                          for j in range(8):
                              dg = diag[j % 2]
                              S.op(S.dve, lambda: nc.vector.tensor_scalar(out=dg.t[:, :], in0=identf.t[:, :],
                                                                          scalar1=modcol.t[:, gc0 + j:gc0 + j + 1], scalar2=None,
                                                                          op0=ALU.mult), reads=[identf.b, modcol.b], writes=[dg.b])
                              bank = pb[1 + j // 4]
                              S.op(S.pe, lambda: nc.tensor.matmul(bank.t[:, (j % 4) * 128:(j % 4) * 128 + 128], lhsT=onesf.t[:, :],
                                                                  rhs=dg.t[:, :], start=True, stop=True, skip_group_check=True),
                                   reads=[dg.b, onesf.b], writes=[bank.b])
                          for hb in range(2):
                              S.op(S.act, lambda: nc.scalar.copy(out=gb.t[:, hb * 512:(hb + 1) * 512], in_=pb[1 + hb].t[:, :]),
                                   reads=[pb[1 + hb].b], writes=[gb.b], par=True)
                      pieces1 = [("win", k) for k in range(8)]
                      pieces2 = [("ec", h) for h in range(6)] + [("wout", k) for k in range(8)] + [("wfo", j) for j in range(NJ)] + \
                                [("wfi", k, g_, hf_) for k in range(8) for g_ in range(2) for hf_ in range(2)]

                      def w_load(pc_, stg):
                          if pc_[0] == "ec":
                              for j in range(2):
                                  S.dma(stg.t[64 * j:64 * j + 64, 0:896].rearrange("p (r c) -> p r c", c=64),
                                        rpbt[l, pc_[1], j:j + 14, :, :].rearrange("r k c -> k r c"), writes=[stg.b], par=(j > 0))
                          elif pc_[0] == "win":
                              S.dma(stg.t[:, 0:PW], w_in[l, pc_[1] * 128:(pc_[1] + 1) * 128, :], writes=[stg.b])
                          elif pc_[0] == "wout":
                              S.dma(stg.t[:, 0:D_], w_out[l, pc_[1] * 128:(pc_[1] + 1) * 128, :], writes=[stg.b])
                          elif pc_[0] == "wfo":
                              S.dma(stg.t[:, 0:D_], w_fo[l, pc_[1] * 128:(pc_[1] + 1) * 128, :], writes=[stg.b])
                          else:
                              c0_ = pc_[2] * DFF + pc_[3] * 1408
                              S.dma(stg.t[:, 0:1408], w_fi[l, pc_[1] * 128:(pc_[1] + 1) * 128, c0_:c0_ + 1408], writes=[stg.b])

                      def w_proc(pc_, stg, ob):
                          if pc_[0] == "ec":
                              h = pc_[1]
                              S.op(S.act, lambda: nc.scalar.activation(out=ob.t[:, 0:896], in_=stg.t[:, 0:896], func=AF.Exp), reads=[stg.b], writes=[ob.b])
                              S.dma(ec_scr[l, :, h * 896:(h + 1) * 896], ob.t[:, 0:896], reads=[ob.b], writes=[ecb], par=True, own=ob.b)
                          elif pc_[0] == "win":
                              k = pc_[1]
                              S.op(S.act, lambda: nc.scalar.copy(out=winb.t[:, k, :], in_=stg.t[:, 0:PW]),
                                   reads=[stg.b], writes=[winb.b], par=True)
                          elif pc_[0] == "wout":
                              k = pc_[1]
                              if k in (3, 4):
                                  S.op(S.dve, lambda: nc.vector.tensor_tensor(out=ob.t[:, 0:D_], in0=stg.t[:, 0:D_], in1=g1b.t[:, :],
                                                                              op=ALU.mult), reads=[stg.b, g1b.b], writes=[ob.b])
                              else:
                                  S.op(S.dve, lambda: nc.vector.scalar_tensor_tensor(out=ob.t[:, 0:D_], in0=stg.t[:, 0:D_],
                                                                                     scalar=cols.t[:, cb + 64 + k:cb + 65 + k], in1=g1b.t[:, :],
                                                                                     op0=ALU.mult, op1=ALU.mult),
                                       reads=[stg.b, g1b.b, cols.b], writes=[ob.b])
                              S.dma(wout_scr[l, k * 128:(k + 1) * 128, :], ob.t[:, 0:D_], reads=[ob.b], writes=[woutb], par=True, own=ob.b)
                          elif pc_[0] == "wfo":
                              j = pc_[1]
                              S.op(S.dve, lambda: nc.vector.tensor_tensor(out=ob.t[:, 0:D_], in0=stg.t[:, 0:D_], in1=g2b.t[:, :],
                                                                          op=ALU.mult), reads=[stg.b, g2b.b], writes=[ob.b])
                              S.dma(wo_scr[l, j * 128:(j + 1) * 128, :], ob.t[:, 0:D_], reads=[ob.b], writes=[wob], par=True, own=ob.b)
                          else:
                              k, g_, hf_ = pc_[1], pc_[2], pc_[3]
                              if g_ == 0:
                                  S.op(S.act, lambda: nc.scalar.copy(out=ob.t[:, 0:1408], in_=stg.t[:, 0:1408]), reads=[stg.b], writes=[ob.b])
                              else:
                                  S.op(S.dve, lambda: nc.vector.tensor_copy(out=ob.t[:, 0:1408], in_=stg.t[:, 0:1408]), reads=[stg.b], writes=[ob.b])
                              dst = wi_scr[l].rearrange("j p (g k c) -> p g k j c", g=2, k=8)[:, g_, k, hf_ * 11:(hf_ + 1) * 11, :]
                              S.dma(dst, ob.t[:, 0:1408].rearrange("p (j c) -> p j c", c=128), reads=[ob.b], writes=[wib], par=True, own=ob.b)
                      S.barrier()
                      ck('prep%d' % l)

                  with ExitStack() as p1:
                      xts = [sb(p1, "xt%d" % i, [128, 4, D_], F32) for i in range(2)]
                      hTs = [sb(p1, "hT%d" % i, [128, 8, 512], BF16) for i in range(2)]
                      qst = [sb(p1, "qst%d" % i, [128, 512], BF16) for i in range(4)]
                      vst = [sb(p1, "vst%d" % i, [128, 4, 384], BF16) for i in range(2)]
                      ust = [sb(p1, "ust%d" % i, [128, 512], F32) for i in range(2)]
                      stage2 = [sb(p1, "stage2_%d" % i, [128, 1408], F32) for i in range(3)]
                      obuf2 = [sb(p1, "obuf2_%d" % i, [128, 1408], BF16) for i in range(2)]

                      def prep2_gen():
                          n2 = len(pieces2)
                          w_load(pieces2[0], stage2[0])
                          w_load(pieces2[1], stage2[1])
                          yield
                          for i2 in range(n2):
                              if i2 + 2 < n2:
                                  w_load(pieces2[i2 + 2], stage2[(i2 + 2) % 3])
                              w_proc(pieces2[i2], stage2[i2 % 3], obuf2[i2 % 2])
                              yield
                      pg2 = prep2_gen()
                      qk_cols = [0, 128, 256, 384, 512, 640, 1408, 1536, 1664, 1792, 1920, 2048]
                      is_q = [1, 1, 1, 0, 0, 0, 1, 1, 1, 0, 0, 0]
                      bctr = [0]

                      def nbank():
                          bctr[0] += 1
                          return pb[2 + bctr[0] % 6]
                      def xload(tt_):
                          S.dma(xts[tt_ % 2].t[:, :, :], x_src[tt_ * 512:(tt_ + 1) * 512, :].rearrange("(s p) d -> p s d", p=128),
                                reads=([xsb] if xsb else []), writes=[xts[tt_ % 2].b])
                      def P1_norm(tt):
                          xt = xts[tt % 2]
                          hT = hTs[tt % 2]
                          for s in range(4):
                              S.op(S.act, lambda: nc.scalar.activation(out=junk.t[:, :], in_=xt.t[:, s, :], func=AF.Square,
                                                                       accum_out=ssq.t[:, s:s + 1]),
                                   reads=[xt.b], writes=[junk.b, ssq.b])
                          rstd_from(ssq.t[:, :], rs.t[:, :], D_, [ssq.b], rs.b)
                          for s in range(4):
                              S.op(S.dve, lambda: nc.vector.tensor_scalar(out=xt.t[:, s, :], in0=xt.t[:, s, :],
                                                                          scalar1=rs.t[:, s:s + 1], scalar2=None, op0=ALU.mult),
                                   reads=[rs.b, xt.b], writes=[xt.b], par=True)
                      def P1_norm_gen(tt):
                          xt = xts[tt % 2]
                          for s in range(4):
                              S.op(S.act, lambda: nc.scalar.activation(out=junk.t[:, :], in_=xt.t[:, s, :], func=AF.Square,
                                                                       accum_out=ssq.t[:, s:s + 1]),
                                   reads=[xt.b], writes=[junk.b, ssq.b])
                              yield
                          rstd_from(ssq.t[:, :], rs.t[:, :], D_, [ssq.b], rs.b)
                          yield
                          for s in range(4):
                              S.op(S.dve, lambda: nc.vector.tensor_scalar(out=xt.t[:, s, :], in0=xt.t[:, s, :],
                                                                          scalar1=rs.t[:, s:s + 1], scalar2=None, op0=ALU.mult),
                                   reads=[rs.b, xt.b], writes=[xt.b], par=True)
                              yield
                      ngen = [None]

                      def P1_tr(tt):
                          xt = xts[tt % 2]
                          hT = hTs[tt % 2]
                          for c in range(8):
                              pt = pb[c % 2]

                              def f():
                                  for s in range(4):
                                      i = nc.tensor.transpose(out=pt.t[:, s * 128:(s + 1) * 128],
                                                              in_=xt.t[:, s, c * 128:(c + 1) * 128], identity=identf.t[:, :])
                                  return i
                              S.op(S.pe, f, reads=[xt.b, identf.b], writes=[pt.b])
                              S.op(S.act, lambda: nc.scalar.activation(out=hT.t[:, c, :], in_=pt.t[:, :], func=AF.Identity,
                                                                       bias=modcol.t[:, c:c + 1], scale=A1.t[:, c:c + 1]),
                                   reads=[pt.b, modcol.b, A1.b], writes=[hT.b], par=True)
                      def P1_qk(tt):
                          xt = xts[tt % 2]
                          hT = hTs[tt % 2]
                          for ci, col in enumerate(qk_cols):
                              pp = nbank()

                              def f():
                                  for k in range(8):
                                      i = nc.tensor.matmul(pp.t[:, :], lhsT=winb.t[:, k, col:col + 128], rhs=hT.t[:, k, :],
                                                           start=(k == 0), stop=(k == 7))
                                  return i
                              S.op(S.pe, f, reads=[winb.b, hT.b], writes=[pp.b])
                              st = qst[ci % 4]
                              evac(st.t[:, :], pp.t[:, :], [pp.b], [st.b], scale=(0.125 if is_q[ci] else 1.0), par=False)
                              S.dma(qk_scr[ci, :, tt * 512:(tt + 1) * 512], st.t[:, :], reads=[st.b], writes=[qkb[ci]], par=True, own=st.b)
                              if ci % 2 == 1:
                                  next(pg2, None)
                              if ngen[0] is not None and ci >= 2:
                                  next(ngen[0], None)
                      def P1_v(tt):
                          xt = xts[tt % 2]
                          hT = hTs[tt % 2]
                          for mi, vcol in enumerate((768, 2176)):
                              vs = vst[mi]
                              for s in range(4):
                                  pp = nbank()

                                  def f():
                                      for k in range(8):
                                          i = nc.tensor.matmul(pp.t[:, 0:384], lhsT=hT.t[:, k, s * 128:(s + 1) * 128],
                                                               rhs=winb.t[:, k, vcol:vcol + 384], start=(k == 0), stop=(k == 7))
                                      return i
                                  S.op(S.pe, f, reads=[winb.b, hT.b], writes=[pp.b])
                                  evac(vs.t[:, s, :], pp.t[:, 0:384], [pp.b], [vs.b], par=True)
                                  if s % 2 == 1:
                                      next(pg2, None)
                              S.dma(v_scr[mi, tt * 512:(tt + 1) * 512, :].rearrange("(s p) f -> p s f", p=128), vs.t[:, :, :],
                                    reads=[vs.b], writes=[vb[mi]], par=True, own=vs.b)
                          for ui, ucol in enumerate((1152, 1280)):
                              pp = nbank()

                              def f():
                                  for k in range(8):
                                      i = nc.tensor.matmul(pp.t[:, :], lhsT=winb.t[:, k, ucol:ucol + 128], rhs=hT.t[:, k, :],
                                                           start=(k == 0), stop=(k == 7))
                                  return i
                              S.op(S.pe, f, reads=[winb.b, hT.b], writes=[pp.b])
                              evac(ust[ui].t[:, :], pp.t[:, :], [pp.b], [ust[ui].b], par=False)
                              S.dma(u_scr[ui, :, tt * 512:(tt + 1) * 512], ust[ui].t[:, :], reads=[ust[ui].b], writes=[ub[ui]], par=True, own=ust[ui].b)
                      xload(0)
                      xload(1)
                      P1_norm(0)
                      P1_tr(0)
                      for tt in range(8):
                          ngen[0] = P1_norm_gen(tt + 1) if tt + 1 < 8 else None
                          P1_qk(tt)
                          if ngen[0] is not None:
                              for _ in ngen[0]:
                                  pass
                          P1_v(tt)
                          if tt + 1 < 8:
                              P1_tr(tt + 1)
                          if tt + 2 < 8:
                              xload(tt + 2)
                      for _ in pg2:
                          pass
                      S.barrier()
                      ck('p1_%d' % l)

              with ExitStack() as pa:
                  QT = sb(pa, "QT", [128, S_], BF16)
                  KT = sb(pa, "KT", [128, S_], BF16)
                  Vd = [sb(pa, "Vd%d" % i, [128, 32, 128], BF16) for i in range(3)]
                  QTd = [None, sb(pa, "QT4", [128, S_], BF16), sb(pa, "QT16", [128, S_], BF16)]
                  KTd = [None, sb(pa, "KT4", [128, S_], BF16), sb(pa, "KT16", [128, S_], BF16)]
                  estg = [sb(pa, "estg%d" % i, [128, 512], F32) for i in range(2)]
                  ectr = [0]
                  rfin = [sb(pa, "rfin%d" % i, [128, 512], F32) for i in range(4)]
                  accO = sb(pa, "accO", [128, S_], F32)
                  accD = sb(pa, "accD", [128, S_], F32)
                  BA = sb(pa, "BA", [128, 3, 6, 256], BF16)
                  sbias = [sb(pa, "sbias%d" % i, [128, 2, 256], BF16) for i in range(4)]
                  S.dma(BA.t[:, :, :, :], ba_in, writes=[BA.b], E=S.pool)
                  PT = [sb(pa, "PT%d" % i, [128, 2, 256], BF16) for i in range(4)]
                  kctr = 0
                  def a_loads(p_, which):
                      if "qk" in which:
                          S.dma(QT.t[:, :], qk_scr[p_], reads=[qkb[p_]], writes=[QT.b])
                          S.dma(KT.t[:, :], qk_scr[3 + p_], reads=[qkb[3 + p_]], writes=[KT.b])
                      for bi, d in enumerate((1, 4, 16)):
                          if ("v%d" % bi) not in which:
                              continue
                          src = v_scr[0][:, p_ * 128:(p_ + 1) * 128].rearrange("(jb q r) f -> q r jb f", r=d, q=128)
                          for r in range(d):
                              nkb_ = 32 // d
                              S.dma(Vd[bi].t[:, r * nkb_:(r + 1) * nkb_, :], src[:, r, :, :], reads=[vb[0]], writes=[Vd[bi].b],
                                    par=(r > 0))
                  a_loads(0, ("qk", "v0", "v1", "v2"))
                  for p in range(3):
                      if p > 0:
                          a_loads(p, ("v2",))
                      QTd[0], KTd[0] = QT, KT

                      def deint(bi, d, c):
                          w_ = 512 // d
                          for (src_, dst_) in ((QT, QTd[bi]), (KT, KTd[bi])):
                              S.op(S.act, lambda: nc.scalar.copy(out=dst_.t[:, :].rearrange("p (r i) -> p r i", r=d)[:, :, c * w_:(c + 1) * w_],
                                                                 in_=src_.t[:, c * 512:(c + 1) * 512].rearrange("p (i r) -> p r i", r=d)),
                                   reads=[src_.b], writes=[dst_.b], par=(c > 0))

                      def norm_chunk(pp_, c):
                          ts_ = slice(c * 512, (c + 1) * 512)
                          S.op(S.dve, lambda: nc.vector.reciprocal(out=accD.t[:, ts_], in_=accD.t[:, ts_]), reads=[accD.b], writes=[accD.b], par=True)
                          S.op(S.dve, lambda: nc.vector.tensor_tensor(out=mixT.t[:, pp_, ts_], in0=accO.t[:, ts_], in1=accD.t[:, ts_], op=ALU.mult),
                               reads=[accO.b, accD.b], writes=[mixT.b], par=True)
                      tasks = []
                      for bi, d in enumerate((1, 4, 16)):
                          for r in range(d):
                              for jb in range((S_ // d) // 128):
                                  tasks.append((bi, d, r, jb))
                      started = {}
                      segbank = {}

                      def stA(task):
                          nonlocal kctr
                          bi, d, r, jb = task
                          n = S_ // d
                          q0 = max(0, jb * 128 - 64)
                          q1 = min(n, jb * 128 + 192)
                          c0 = q0 - (jb * 128 - 64)
                          c1 = c0 + (q1 - q0)
                          kctr += 1
                          b0 = 2 * (kctr % 3)
                          spbs = [pb[b0], pb[b0 + 1]]
                          pt_ = PT[kctr % 4]

                          def tsl(a, b):
                              return slice(r + d * a, r + d * (b - 1) + 1, d)
                          qt_, kt_ = QTd[bi], KTd[bi]
                          ro = r * n

                          def f():
                              for hh in range(2):
                                  i = nc.tensor.matmul(spbs[hh].t[:, c0:c1],
                                                       lhsT=kt_.t[hh * 64:(hh + 1) * 64, ro + jb * 128:ro + jb * 128 + 128],
                                                       rhs=qt_.t[hh * 64:(hh + 1) * 64, ro + q0:ro + q1],
                                                       start=True, stop=True, skip_group_check=True)
                              return i
                          S.op(S.pe, f, reads=[kt_.b, qt_.b], writes=[spbs[0].b, spbs[1].b])
                          sbs = sbias[kctr % 4]
                          S.op(S.act, lambda: nc.scalar.activation(out=sbs.t[:, :, c0:c1], in_=ps3[:, b0:b0 + 2, c0:c1], func=AF.Exp),
                               reads=[spbs[0].b, spbs[1].b], writes=[sbs.b])
                          S.op(S.dve, lambda: nc.vector.tensor_tensor(out=pt_.t[:, :, c0:c1], in0=sbs.t[:, :, c0:c1],
                                                                      in1=BA.t[:, bi, 2 * p:2 * p + 2, c0:c1], op=ALU.mult),
                               reads=[sbs.b, BA.b], writes=[pt_.b])
                          return (task, q0, q1, c0, pt_, tsl)

                      def stB(ctx):
                          (bi, d, r, jb), q0, q1, c0, pt_, tsl = ctx
                          n = S_ // d
                          nkb = n // 128
                          seg = 256
                          nseg = n // seg
                          kps = seg // 128
                          st_ = started.setdefault((bi, r), set())

                          def sbank(sg):
                              key = (bi, r, sg)
                              if key not in segbank:
                                  segbank[key] = pb[6 + len(segbank) % 2]
                              return segbank[key]
                          pieces = []
                          qa = q0
                          while qa < q1:
                              sg = qa // seg
                              qb_ = min(q1, (sg + 1) * seg)
                              pieces.append((sg, qa, qb_))
                              qa = qb_
                          banks = []
                          for (sg, _, _) in pieces:
                              banks += [sbank(sg).b]

                          def g():
                              for (sg, qa, qb_) in pieces:
                                  jlast = min(nkb - 1, (sg + 1) * kps)
                                  for hh in range(2):
                                      first = (sg, hh) not in st_
                                      st_.add((sg, hh))
                                      last = (jb == jlast)
                                      rhs = pt_.t[:, hh, c0 + (qa - q0):c0 + (qb_ - q0)]
                                      nc.tensor.matmul(sbank(sg).t[hh * 64:(hh + 1) * 64, qa - sg * seg:qb_ - sg * seg],
                                                       lhsT=Vd[bi].t[:, r * nkb + jb, hh * 64:(hh + 1) * 64], rhs=rhs,
                                                       start=first, stop=False, skip_group_check=True)
                                      i = nc.tensor.matmul(sbank(sg).t[hh * 64:(hh + 1) * 64, 256 + qa - sg * seg:256 + qb_ - sg * seg],
                                                           lhsT=onesb.t[:, 0:64], rhs=rhs,
                                                           start=False, stop=last, skip_group_check=True)
                              return i
                          S.op(S.pe, g, reads=[pt_.b, Vd[bi].b, onesb.b], writes=banks)
                          for sg in range(nseg):
                              if jb == min(nkb - 1, (sg + 1) * kps):
                                  ts_ = tsl(sg * seg, (sg + 1) * seg)
                                  ob_ = sbank(sg)
                                  if bi == 0:
                                      S.op(S.act, lambda: nc.scalar.copy(out=accO.t[:, ts_], in_=ob_.t[:, 0:seg]),
                                           reads=[ob_.b], writes=[accO.b], par=True)
                                      S.op(S.act, lambda: nc.scalar.copy(out=accD.t[:, ts_], in_=ob_.t[:, 256:256 + seg]),
                                           reads=[ob_.b], writes=[accD.b], par=True)
                                  else:
                                      ectr[0] += 1
                                      es_ = estg[ectr[0] % 2]
                                      S.op(S.act, lambda: nc.scalar.copy(out=es_.t[:, :], in_=ob_.t[:, :]), reads=[ob_.b], writes=[es_.b])
                                      S.op(S.pool, lambda: nc.gpsimd.tensor_tensor(out=accO.t[:, ts_], in0=es_.t[:, 0:seg],
                                                                                   in1=accO.t[:, ts_], op=ALU.add),
                                           reads=[es_.b, accO.b], writes=[accO.b], par=True)
                                      S.op(S.pool, lambda: nc.gpsimd.tensor_tensor(out=accD.t[:, ts_], in0=es_.t[:, 256:256 + seg],
                                                                                   in1=accD.t[:, ts_], op=ALU.add),
                                           reads=[es_.b, accD.b], writes=[accD.b], par=True)

                      ctxs = [stA(tasks[0]), stA(tasks[1])]
                      for ti in range(2, len(tasks)):
                          if p > 0 and ti % 2 == 0 and 2 <= ti <= 16:
                              norm_chunk(p - 1, (ti - 2) // 2)
                          if ti % 2 == 1 and 3 <= ti <= 17:
                              deint(1, 4, (ti - 3) // 2)
                          if ti % 2 == 1 and 33 <= ti <= 47:
                              deint(2, 16, (ti - 33) // 2)
                          if ti == 50 and p + 1 < 3:
                              a_loads(p + 1, ("qk", "v0"))
                          if ti == 72 and p + 1 < 3:
                              a_loads(p + 1, ("v1",))
                          ctxs.append(stA(tasks[ti]))
                          stB(ctxs[ti - 2])
                      stB(ctxs[-2])
                      stB(ctxs[-1])
                      if p == 2:
                          for c_ in range(8):
                              ts_ = slice(c_ * 512, (c_ + 1) * 512)
                              ra, rb_ = rfin[c_ % 2], rfin[2 + c_ % 2]
                              S.op(S.act, lambda: nc.scalar.activation(out=ra.t[:, :], in_=accD.t[:, ts_], func=AF.Ln), reads=[accD.b], writes=[ra.b])
                              S.op(S.act, lambda: nc.scalar.activation(out=rb_.t[:, :], in_=ra.t[:, :], func=AF.Exp, scale=-1.0),
                                   reads=[ra.b], writes=[rb_.b])
                              S.op(S.dve, lambda: nc.vector.scalar_tensor_tensor(out=ra.t[:, :], in0=accD.t[:, ts_], scalar=-1.0, in1=rb_.t[:, :],
                                                                                 op0=ALU.mult, op1=ALU.mult),
                                   reads=[accD.b, rb_.b], writes=[ra.b])
                              S.op(S.dve, lambda: nc.vector.scalar_tensor_tensor(out=ra.t[:, :], in0=ra.t[:, :], scalar=2.0, in1=rb_.t[:, :],
                                                                                 op0=ALU.add, op1=ALU.mult),
                                   reads=[ra.b, rb_.b], writes=[ra.b])
                              S.op(S.dve, lambda: nc.vector.tensor_tensor(out=mixT.t[:, p, ts_], in0=accO.t[:, ts_], in1=ra.t[:, :], op=ALU.mult),
                                   reads=[accO.b, ra.b], writes=[mixT.b], par=True)
                  S.barrier()
                  ck('p2a_%d' % l)


              def pool_gen(U, T1, T2, invc, pd, wpf, wpb):
                  PAD = 16
                  L = S_ + 2 * PAD
                  for tz in (U, T1, T2):
                      S.op(S.pool, lambda: nc.gpsimd.memset(tz.t[:, :], 0.0), writes=[tz.b])
                  yield
                  for ch in range(2):
                      S.dma(U.t[:, PAD:PAD + S_], u_scr[ch], reads=[ub[ch]], writes=[U.b])
                      S.dma(invc.t[:, :], inv_in[ch], writes=[invc.b])
                      S.op(S.pool, lambda: nc.gpsimd.memset(wpf.t[:, :], 0.0), writes=[wpf.b])
                      for g2_ in range(2):
                          S.dma(wpf.t[g2_ * 64:(g2_ + 1) * 64, g2_ * 64:(g2_ + 1) * 64], w_pool[l, ch * 2 + g2_], writes=[wpf.b])
                      yield
                      S.op(S.act, lambda: nc.scalar.copy(out=wpb.t[:, :], in_=wpf.t[:, :]), reads=[wpf.b], writes=[wpb.b])
                      yield

                      def shadd(dst, src, sh, prt):
                          S.op(S.dve, lambda: nc.vector.tensor_tensor(out=dst.t[prt, 0:L - sh], in0=src.t[prt, 0:L - sh],
                                                                      in1=src.t[prt, sh:L], op=ALU.add),
                               reads=[src.b], writes=[dst.b])
                      allp = slice(0, 128)
                      hi_ = slice(64, 128)
                      if ch == 0:
                          shadd(T1, U, 1, allp)
                          yield
                          shadd(T2, T1, 2, hi_)
                          yield
                          fin = ((T1, slice(0, 64), 1), (T2, hi_, 2))
                      else:
                          shadd(T1, U, 1, allp)
                          yield
                          shadd(T2, T1, 2, allp)
                          yield
                          shadd(T1, T2, 4, allp)
                          yield
                          shadd(T2, T1, 8, hi_)
                          yield
                          fin = ((T1, slice(0, 64), 4), (T2, hi_, 8))
                      for (src, prt, hw) in fin:
                          S.op(S.dve, lambda: nc.vector.tensor_tensor(out=invc.t[prt, :], in0=src.t[prt, PAD - hw:PAD - hw + S_],
                                                                      in1=invc.t[prt, :], op=ALU.mult),
                               reads=[src.b, invc.b], writes=[invc.b], par=True)
                          yield
                      S.op(S.dve, lambda: nc.vector.tensor_tensor(out=pd.t[:, :], in0=invc.t[:, :], in1=U.t[:, PAD:PAD + S_],
                                                                  op=ALU.subtract), reads=[invc.b, U.b], writes=[pd.b])
                      yield
                      for tt in range(8):
                          pp = pb[4 + tt % 2]
                          S.op(S.pe, lambda: nc.tensor.matmul(pp.t[:, :], lhsT=wpb.t[:, :], rhs=pd.t[:, tt * 512:(tt + 1) * 512],
                                                              start=True, stop=True, skip_group_check=True), reads=[wpb.b, pd.b], writes=[pp.b])
                          S.op(S.act, lambda: nc.scalar.activation(out=mixT.t[:, 3 + ch, tt * 512:(tt + 1) * 512], in_=pp.t[:, :],
                                                                   func=AF.Identity, scale=cols.t[:, cb + 64 + 3 + ch:cb + 64 + 4 + ch]),
                               reads=[pp.b, cols.b], writes=[mixT.b], par=True)
                          yield

              with ExitStack() as pc:
                  QT = sb(pc, "QTc", [128, S_], BF16)
                  KT = sb(pc, "KTc", [128, S_], BF16)
                  Ve = sb(pc, "Ve", [128, 32, 128], BF16)
                  Vo = sb(pc, "Vo", [128, 32, 128], BF16)
                  sbias = [sb(pc, "sbc%d" % i, [128, 512], BF16) for i in range(4)]
                  EC = sb(pc, "EC", [128, 6, 14, 64], BF16)
                  PT = [sb(pc, "PTc%d" % i, [128, 512], BF16) for i in range(4)]
                  rden = [sb(pc, "rden%d" % i, [128, 256], F32) for i in range(2)]
                  rtmp_c = [sb(pc, "rtmpc%d" % i, [128, 256], F32) for i in range(2)]
                  def c_loads(p_):
                      S.dma(QT.t[:, :], qk_scr[6 + p_], reads=[qkb[6 + p_]], writes=[QT.b])
                      S.dma(KT.t[:, :], qk_scr[9 + p_], reads=[qkb[9 + p_]], writes=[KT.b])
                      vsrc = v_scr[1][:, p_ * 128:(p_ + 1) * 128]
                      S.dma(Ve.t[:, :, :], vsrc.rearrange("(t q) f -> q t f", q=128), reads=[vb[1]], writes=[Ve.b])
                      S.dma(Vo.t[:, 0:31, :], vsrc[64:64 + 31 * 128, :].rearrange("(t q) f -> q t f", q=128), reads=[vb[1]], writes=[Vo.b])
                  c_loads(0)
                  cmk = sb(pc, "cmk", [128, 14, 64], F32)
                  S.dma(cmk.t[:, :, :], cm_in, writes=[cmk.b])
                  S.dma(EC.t[:, :, :, :], ec_scr[l].rearrange("p (h r c) -> p h r c", h=6, r=14), reads=[ecb], writes=[EC.b])
                  for h in range(6):
                      S.op(S.dve, lambda: nc.vector.tensor_tensor(out=EC.t[:, h, :, :], in0=EC.t[:, h, :, :], in1=cmk.t[:, :, :], op=ALU.mult),
                           reads=[EC.b, cmk.b], writes=[EC.b], par=True)
                  PADp = 16
                  ptiles = (sb(pc, "U", [128, S_ + 2 * PADp], F32), sb(pc, "T1", [128, S_ + 2 * PADp], F32),
                            sb(pc, "T2", [128, S_ + 2 * PADp], F32), sb(pc, "invc", [128, S_], F32), sb(pc, "pd", [128, S_], BF16),
                            sb(pc, "wpf", [128, 128], F32), sb(pc, "wpb", [128, 128], BF16))
                  pgen = pool_gen(*ptiles)
                  kctr = 0
                  for p in range(3):
                      if p > 0:
                          c_loads(p)
                      def cA(r):
                          nonlocal kctr
                          rs_ = min(max(r - 4, 0), 56)
                          dr0 = rs_ - r + 7
                          kctr += 1
                          b0 = 2 * (kctr % 3)
                          spbs = [pb[b0], pb[b0 + 1]]
                          pt_ = PT[kctr % 4]

                          def f():
                              for hh in range(2):
                                  for i4 in range(4):
                                      a = rs_ + 2 * i4
                                      i = nc.tensor.matmul(spbs[hh].t[:, i4 * 64:i4 * 64 + 64],
                                                           lhsT=KT.t[hh * 64:(hh + 1) * 64, a * 64:a * 64 + 128],
                                                           rhs=QT.t[hh * 64:(hh + 1) * 64, r * 64:r * 64 + 64],
                                                           start=True, stop=True, skip_group_check=True)
                              return i
                          S.op(S.pe, f, reads=[KT.b, QT.b], writes=[spbs[0].b, spbs[1].b])
                          sbs = sbias[kctr % 4]
                          S.op(S.act, lambda: nc.scalar.activation(out=sbs.t[:, :].rearrange("p (h c) -> p h c", h=2),
                                                                   in_=ps3[:, b0:b0 + 2, 0:256], func=AF.Exp),
                               reads=[spbs[0].b, spbs[1].b], writes=[sbs.b])
                          S.op(S.dve, lambda: nc.vector.tensor_tensor(
                              out=pt_.t[:, :].rearrange("p (h i c) -> p h i c", h=2, i=4),
                              in0=sbs.t[:, :].rearrange("p (h i c) -> p h i c", h=2, i=4),
                              in1=EC.t[:, 2 * p:2 * p + 2, dr0:dr0 + 7:2, :], op=ALU.mult),
                              reads=[sbs.b, EC.b], writes=[pt_.b])
                          return (r, rs_, pt_)

                      def cB(ctx):
                          r, rs_, pt_ = ctx
                          ob_ = pb[6 + (r // 4) % 2]

                          def g():
                              for hh in range(2):
                                  for i4 in range(4):
                                      a = rs_ + 2 * i4
                                      vt = Ve.t[:, a // 2, hh * 64:(hh + 1) * 64] if a % 2 == 0 else Vo.t[:, (a - 1) // 2, hh * 64:(hh + 1) * 64]
                                      rhs = pt_.t[:, hh * 256 + i4 * 64:hh * 256 + i4 * 64 + 64]
                                      nc.tensor.matmul(ob_.t[hh * 64:(hh + 1) * 64, (r % 4) * 64:(r % 4) * 64 + 64], lhsT=vt, rhs=rhs,
                                                       start=(i4 == 0), stop=False, skip_group_check=True)
                                      i = nc.tensor.matmul(ob_.t[hh * 64:(hh + 1) * 64, 256 + (r % 4) * 64:256 + (r % 4) * 64 + 64],
                                                           lhsT=onesb.t[:, 0:64], rhs=rhs,
                                                           start=False, stop=(i4 == 3), skip_group_check=True)
                              return i
                          S.op(S.pe, g, reads=[pt_.b, Ve.b, Vo.b, onesb.b], writes=[ob_.b])
                          if r % 4 == 3:
                              rd = rden[(r // 4) % 2]
                              tn = rtmp_c[(r // 4) % 2]
                              den_ = ob_.t[:, 256:512]
                              S.op(S.act, lambda: nc.scalar.activation(out=tn.t[:, :], in_=den_, func=AF.Ln), reads=[ob_.b], writes=[tn.b])
                              S.op(S.act, lambda: nc.scalar.activation(out=rd.t[:, :], in_=tn.t[:, :], func=AF.Exp, scale=-1.0),
                                   reads=[tn.b], writes=[rd.b])
                              S.op(S.dve, lambda: nc.vector.scalar_tensor_tensor(out=tn.t[:, :], in0=den_, scalar=-1.0, in1=rd.t[:, :],
                                                                                 op0=ALU.mult, op1=ALU.mult),
                                   reads=[ob_.b, rd.b], writes=[tn.b])
                              S.op(S.dve, lambda: nc.vector.scalar_tensor_tensor(out=tn.t[:, :], in0=tn.t[:, :], scalar=2.0, in1=rd.t[:, :],
                                                                                 op0=ALU.add, op1=ALU.mult),
                                   reads=[tn.b, rd.b], writes=[tn.b])
                              S.op(S.dve, lambda: nc.vector.tensor_tensor(out=mixT.t[:, 5 + p, (r - 3) * 64:(r + 1) * 64],
                                                                          in0=ob_.t[:, 0:256], in1=tn.t[:, :], op=ALU.mult),
                                   reads=[ob_.b, tn.b], writes=[mixT.b], par=True)

                      cctx = [cA(0), cA(1)]
                      for r in range(2, 64):
                          cctx.append(cA(r))
                          cB(cctx[r - 2])
                          if r % 3 == 0:
                              next(pgen, None)
                      cB(cctx[-2])
                      cB(cctx[-1])
                  for _ in pgen:
                      pass
                  S.barrier()
                  ck('p2c_%d' % l)

              with ExitStack() as pn:
                  if dbg:
                      for c in range(8):
                          S.dma(mix_dbg[l, c], mixT.t[:, c, :], reads=[mixT.b], writes=[mixdbgb], par=True, own=mixT.b)
                  S.barrier()
                  ck('norm_%d' % l)

              with ExitStack() as p3:
                  woutb_sb = sb(p3, "wout_sb", [128, 8, D_], BF16)
                  xt3s = [sb(p3, "xt3_%d" % i, [128, 4, D_], F32) for i in range(2)]
                  xn3 = sb(p3, "xn3", [128, 4, D_], F32)
                  hT3s = [sb(p3, "hT3_%d" % i, [128, 8, 512], BF16) for i in range(2)]
                  actb = sb(p3, "actb", [128, NJ, 512], BF16)
                  wis = [sb(p3, "wis%d" % i, [128, 2, 8, 128], BF16) for i in range(3)]
                  wos = [sb(p3, "wos%d" % i, [128, D_], BF16) for i in range(3)]
                  sil = [sb(p3, "sil%d" % i, [128, 512], F32) for i in range(2)]
                  osq = [sb(p3, "osq%d" % i, [128, 3, 512], BF16) for i in range(2)]
                  rsm = sb(p3, "rsm", [128, 8], F32)
                  S.dma(woutb_sb.t[:, :, :], wout_scr[l].rearrange("(k p) n -> p k n", p=128), reads=[woutb], writes=[woutb_sb.b])
                  wlist = []
                  for tt_ in range(8):
                      wlist += [("wi", j_) for j_ in range(NJ)] + [("wo", j_) for j_ in range(NJ)]
                  wstate = [0]

                  def wneed(i):
                      while wstate[0] < len(wlist) and wstate[0] <= i + 2:
                          n_ = wstate[0]
                          kind, j_ = wlist[n_]
                          if kind == "wi":
                              wi_ = wis[n_ % 3]
                              S.dma(wi_.t[:, :, :, :], wi_scr[l, j_].rearrange("p (g k c) -> p g k c", g=2, k=8), reads=[wib], writes=[wi_.b])
                          else:
                              wo_ = wos[n_ % 3]
                              S.dma(wo_.t[:, :], wo_scr[l, j_ * 128:(j_ + 1) * 128, :], reads=[wob], writes=[wo_.b])
                          wstate[0] += 1

                  def xload3(tt_):
                      S.dma(xt3s[tt_ % 2].t[:, :, :], x_src[tt_ * 512:(tt_ + 1) * 512, :].rearrange("(s p) d -> p s d", p=128),
                            reads=([xsb] if xsb else []), writes=[xt3s[tt_ % 2].b])
                  xload3(0)
                  wi_idx = 0
                  def X_a(tt):
                      nonlocal wi_idx
                      tsl_ = slice(tt * 512, (tt + 1) * 512)
                      xt = xt3s[tt % 2]
                      for mi_, c0_ in enumerate((0, 5)):
                          oq = osq[mi_]
                          S.op(S.act, lambda: nc.scalar.activation(out=oq.t[:, :, :], in_=mixT.t[:, c0_:c0_ + 3, tsl_], func=AF.Square),
                               reads=[mixT.b], writes=[oq.b])

                          def f():
                              for s in range(4):
                                  for c in range(3):
                                      i = nc.tensor.matmul(pb[6].t[:, (mi_ * 4 + s) * 2:(mi_ * 4 + s) * 2 + 2],
                                                           lhsT=oq.t[:, c, s * 128:(s + 1) * 128], rhs=onesb.t[:, 0:2],
                                                           start=(c == 0), stop=(c == 2), skip_group_check=True)
                              return i
                          S.op(S.pe, f, reads=[oq.b, onesb.b], writes=[pb[6].b])
                      rstd_from(pb[6].t[:, 0:16].rearrange("p (a q) -> p a q", q=2)[:, :, 0], rsm.t[:, :], 384.0, [pb[6].b], rsm.b)
                      for s in range(4):
                          for nh in range(2):
                              bb = ((s * 2 + nh) % 2) * 3
                              pA, pB, pC = pb[bb], pb[bb + 1], pb[bb + 2]

                              def f():
                                  for (pq, ks) in ((pA, (0, 1, 2)), (pB, (3, 4)), (pC, (5, 6, 7))):
                                      for k in ks:
                                          i = nc.tensor.matmul(pq.t[:, :], lhsT=mixT.t[:, k, tt * 512 + s * 128:tt * 512 + (s + 1) * 128],
                                                               rhs=woutb_sb.t[:, k, nh * 512:(nh + 1) * 512], start=(k == ks[0]), stop=(k == ks[-1]))
                                  return i
                              S.op(S.pe, f, reads=[mixT.b, woutb_sb.b], writes=[pA.b, pB.b, pC.b])
                              xs_ = xt.t[:, s, nh * 512:(nh + 1) * 512]
                              S.op(S.dve, lambda: nc.vector.scalar_tensor_tensor(out=xs_, in0=pA.t[:, :], scalar=rsm.t[:, s:s + 1], in1=xs_,
                                                                                 op0=ALU.mult, op1=ALU.add),
                                   reads=[pA.b, rsm.b, xt.b], writes=[xt.b], par=True)
                              S.op(S.dve, lambda: nc.vector.tensor_tensor(out=xs_, in0=pB.t[:, :], in1=xs_, op=ALU.add),
                                   reads=[pB.b, xt.b], writes=[xt.b], par=True)
                              S.op(S.dve, lambda: nc.vector.scalar_tensor_tensor(out=xs_, in0=pC.t[:, :], scalar=rsm.t[:, 4 + s:5 + s], in1=xs_,
                                                                                 op0=ALU.mult, op1=ALU.add),
                                   reads=[pC.b, rsm.b, xt.b], writes=[xt.b], par=True)
                      for s in range(4):
                          S.op(S.act, lambda: nc.scalar.activation(out=junk.t[:, :], in_=xt.t[:, s, :], func=AF.Square,
                                                                   accum_out=ssq.t[:, s:s + 1]),
                               reads=[xt.b], writes=[junk.b, ssq.b])
                      rstd_from(ssq.t[:, :], rs.t[:, :], D_, [ssq.b], rs.b)
                      for s in range(4):
                          S.op(S.dve, lambda: nc.vector.tensor_scalar(out=xn3.t[:, s, :], in0=xt.t[:, s, :],
                                                                      scalar1=rs.t[:, s:s + 1], scalar2=None, op0=ALU.mult),
                               reads=[rs.b, xt.b], writes=[xn3.b], par=(s > 0))
                  def X_b(tt):
                      nonlocal wi_idx
                      tsl_ = slice(tt * 512, (tt + 1) * 512)
                      xt = xt3s[tt % 2]
                      for c in range(8):
                          pt = pb[6 + c % 2]

                          def f():
                              for s in range(4):
                                  i = nc.tensor.transpose(out=pt.t[:, s * 128:(s + 1) * 128], in_=xn3.t[:, s, c * 128:(c + 1) * 128],
                                                          identity=identf.t[:, :])
                              return i
                          S.op(S.pe, f, reads=[xn3.b, identf.b], writes=[pt.b])
                          S.op(S.act, lambda: nc.scalar.activation(out=hT3s[tt % 2].t[:, c, :], in_=pt.t[:, :], func=AF.Identity,
                                                                   bias=modcol.t[:, 24 + c:25 + c], scale=A2.t[:, c:c + 1]),
                               reads=[pt.b, modcol.b, A2.b], writes=[hT3s[tt % 2].b], par=True)
                  def Y_a(tt):
                      nonlocal wi_idx
                      tsl_ = slice(tt * 512, (tt + 1) * 512)
                      xt = xt3s[tt % 2]
                      for j in range(NJ):
                          wneed(wi_idx)
                          wi = wis[wi_idx % 3]
                          wi_idx += 1
                          pg, pu = pb[4 + (j % 2) * 2], pb[5 + (j % 2) * 2]

                          def f():
                              for gi_, pq in enumerate((pg, pu)):
                                  for k in range(8):
                                      i = nc.tensor.matmul(pq.t[:, :], lhsT=wi.t[:, gi_, k, :], rhs=hT3s[tt % 2].t[:, k, :], start=(k == 0), stop=(k == 7))
                              return i
                          S.op(S.pe, f, reads=[wi.b, hT3s[tt % 2].b], writes=[pg.b, pu.b])
                          sl = sil[j % 2]
                          S.op(S.act, lambda: nc.scalar.activation(out=sl.t[:, :], in_=pg.t[:, :], func=AF.Silu), reads=[pg.b], writes=[sl.b])
                          S.op(S.dve, lambda: nc.vector.tensor_tensor(out=actb.t[:, j, :], in0=pu.t[:, :], in1=sl.t[:, :], op=ALU.mult),
                               reads=[pu.b, sl.b], writes=[actb.b], par=True)
                  def Y_b(tt):
                      nonlocal wi_idx
                      tsl_ = slice(tt * 512, (tt + 1) * 512)
                      xt = xt3s[tt % 2]
                      for j in range(NJ):
                          wneed(wi_idx)
                          wo = wos[wi_idx % 3]
                          wi_idx += 1

                          def f():
                              for s in range(4):
                                  for nh in range(2):
                                      i = nc.tensor.matmul(pb[s * 2 + nh].t[:, :], lhsT=actb.t[:, j, s * 128:(s + 1) * 128],
                                                           rhs=wo.t[:, nh * 512:(nh + 1) * 512], start=(j == 0), stop=(j == NJ - 1))
                              return i
                          S.op(S.pe, f, reads=[wo.b, actb.b], writes=[q_.b for q_ in pb])
                      for s in range(4):
                          for nh in range(2):
                              S.op(S.dve, lambda: nc.vector.tensor_tensor(out=xt.t[:, s, nh * 512:(nh + 1) * 512], in0=pb[s * 2 + nh].t[:, :],
                                                                          in1=xt.t[:, s, nh * 512:(nh + 1) * 512], op=ALU.add),
                                   reads=[pb[s * 2 + nh].b, xt.b], writes=[xt.b], par=True)
                      dst = x_dst[tsl_, :].rearrange("(s p) d -> p s d", p=128)
                      if l == 0:
                          S.dma(dst, xt.t[:, :, :], reads=[xt.b], writes=[xdb], par=True, own=xt.b, E=S.act)
                      else:
                          for s in range(4):
                              S.op(S.act, lambda: nc.scalar.activation(out=junk.t[:, :], in_=xt.t[:, s, :], func=AF.Square,
                                                                       accum_out=ssq.t[:, s:s + 1]),
                                   reads=[xt.b], writes=[junk.b, ssq.b])
                          rstd_from(ssq.t[:, :], rs.t[:, :], D_, [ssq.b], rs.b)
                          for s in range(4):
                              S.op(S.dve, lambda: nc.vector.scalar_tensor_tensor(out=xt.t[:, s, :], in0=xt.t[:, s, :], scalar=rs.t[:, s:s + 1],
                                                                                 in1=nfb.t[:, :], op0=ALU.mult, op1=ALU.mult),
                                   reads=[rs.b, xt.b, nfb.b], writes=[xt.b], par=True)
                          S.dma(dst, xt.t[:, :, :], reads=[xt.b], writes=[xdb], par=True, own=xt.b, E=S.act)
                  X_a(0)
                  X_b(0)
                  for tt in range(8):
                      if tt + 1 < 8:
                          xload3(tt + 1)
                      Y_a(tt)
                      if tt + 1 < 8:
                          X_a(tt + 1)
                      Y_b(tt)
                      if tt + 1 < 8:
                          X_b(tt + 1)

                  S.barrier()
                  ck('p3_%d' % l)
        except _Stop:
            pass
        S.finish()
    return nc


_CONST = None


def _prep_inputs(x, c, w_ada, b_ada, norm_mix, w_in, norm_a_out, norm_c_out, w_pool, pool_scale, rpb, w_out,
                 norm_ffn, w_ffn_in, w_ffn_out, norm_final):
    global _CONST
    if _CONST is None:
        _CONST = host_constants()
    ba, cmask, inv = _CONST
    f = lambda a: np.ascontiguousarray(np.asarray(a, dtype=np.float32))
    col = lambda v: np.asarray(v, np.float32).reshape(-1, 128).T
    kc = np.arange(64)[:, None]
    cc = np.arange(64)[None, :]
    idx = np.clip(kc - cc + 15, 0, 30)
    rpbt = f(np.asarray(rpb, np.float32)[:, :, :, idx])
    nfb = f(np.broadcast_to(np.asarray(norm_final, np.float32)[None, :], (128, D_)))
    shared = dict(nfb=nfb, w_ada=f(w_ada), w_in=f(w_in), w_pool=f(w_pool), rpbt=rpbt, w_out=f(w_out),
                  w_ffn_in=f(w_ffn_in), w_ffn_out=f(w_ffn_out), ba_tab=ba, cmask=cmask, invcnt=inv)
    in_maps = []
    for b in range(8):
        parts = [col(c[b])]
        for l in range(2):
            gains = np.concatenate([np.asarray(norm_a_out[l]), np.asarray(pool_scale[l]), np.asarray(norm_c_out[l])])
            parts += [col(b_ada[l]), col(norm_mix[l]), col(norm_ffn[l]), col(gains)]
        m = dict(shared)
        m["x"] = f(x[b])
        m["colpack"] = f(np.concatenate(parts, axis=1))
        in_maps.append(m)
    return in_maps


def kernel(**inputs):
    in_maps = _prep_inputs(**inputs)
    nc = build_nc()
    res = run_bass_kernel_spmd(nc, in_maps, core_ids=list(range(8)))
    return np.stack([np.asarray(res.results[b]["y"], dtype=np.float32) for b in range(8)], axis=0)
```

```python
import numpy as np
from contextlib import ExitStack
import concourse.bass as bass
import concourse.mybir as mybir
from concourse.bass_utils import run_bass_kernel_spmd

F32 = mybir.dt.float32
BF16 = mybir.dt.bfloat16
AF = mybir.ActivationFunctionType
ALU = mybir.AluOpType

S_ = 4096
D_ = 1024
DFF = 2816
NJ = DFF // 128
PW = 2560
EPS = 1e-6
NEGB = -30000.0
NCOL = 8 + 2 * 72


class Buf:
    __slots__ = ("name", "w", "r", "sem", "cnt")

    def __init__(self, name):
        self.name = name
        self.w = {}
        self.r = {}
        self.sem = None
        self.cnt = 0


class Eng:
    def __init__(self, eng, sem, name, same_wait=True):
        self.eng = eng
        self.sem = sem
        self.cnt = 0
        self.seen = {}
        self.name = name
        self.same_wait = same_wait


class TB:
    def __init__(self, t, name):
        self.t = t
        self.b = Buf(name)


class Sched:
    def __init__(self, nc, es):
        self.nc = nc
        self.es = es
        mk = lambda n: es.enter_context(nc.semaphore(n))
        self.pe = Eng(nc.tensor, mk("s_pe"), "pe", same_wait=False)
        self.act = Eng(nc.scalar, mk("s_act"), "act")
        self.dve = Eng(nc.vector, mk("s_dve"), "dve")
        self.pool = Eng(nc.gpsimd, mk("s_pool"), "pool")
        self.sp = Eng(nc.sync, mk("s_sp"), "sp")
        self.engs = [self.pe, self.act, self.dve, self.pool, self.sp]
        self.dma_bufs = []
        self.nsem = 5
        self.dead = False

    def _wait(self, E, deps):
        for (sem, val, own) in deps:
            if own is E and not E.same_wait:
                continue
            k = id(sem)
            if E.seen.get(k, 0) < val:
                E.eng.wait_ge(sem, val)
                E.seen[k] = val

    @staticmethod
    def _merge(dct, tok):
        k = id(tok[0])
        if k not in dct or dct[k][1] < tok[1]:
            dct[k] = tok

    def _deps(self, reads, writes, par):
        deps = []
        for b in reads:
            deps.extend(b.w.values())
        for b in writes:
            if not par:
                deps.extend(b.w.values())
            deps.extend(b.r.values())
        return deps

    def op(self, E, fn, reads=(), writes=(), par=False):
        if self.dead:
            return None
        self._wait(E, self._deps(reads, writes, par))
        inst = fn()
        E.cnt += 1
        inst.then_inc(E.sem, 1)
        tok = (E.sem, E.cnt, E)
        for b in reads:
            self._merge(b.r, tok)
        for b in writes:
            self._merge(b.w, tok)
        return inst

    def dma(self, out, in_, reads=(), writes=(), par=False, E=None, own=None, **kw):
        if self.dead:
            return
        E = E or self.sp
        self._wait(E, self._deps(reads, writes, par))
        wb = own or writes[0]
        if wb.sem is None:
            wb.sem = self.es.enter_context(self.nc.semaphore("d%d" % self.nsem))
            self.nsem += 1
            self.dma_bufs.append(wb)
        wb.cnt += 16
        E.eng.dma_start(out=out, in_=in_, **kw).then_inc(wb.sem, 16)
        tok = (wb.sem, wb.cnt, None)
        for b in reads:
            self._merge(b.r, tok)
        for b in writes:
            self._merge(b.w, tok)

    def _all(self):
        toks = [(E.sem, E.cnt, None) for E in self.engs if E.cnt > 0]
        toks += [(b.sem, b.cnt, None) for b in self.dma_bufs if b.cnt > 0]
        return toks

    def barrier(self):
        if self.dead:
            return
        toks = self._all()
        for E in self.engs:
            self._wait(E, toks)

    def finish(self):
        self._wait(self.sp, self._all())


def host_constants():
    j = np.arange(128)[:, None]
    c = np.arange(256)[None, :]
    rel = (c - 64) - j
    ok = np.abs(rel) <= 64
    slopes = 2.0 ** (-8.0 * (np.arange(6, dtype=np.float64) + 1.0) / 6.0)
    ba = np.empty((128, 3, 6, 256), np.float32)
    for bi, d in enumerate((1, 4, 16)):
        for h in range(6):
            v = -(slopes[h] * d) * np.abs(rel)
            ba[:, bi, h, :] = np.where(ok, np.exp(v), 0.0).astype(np.float32)
    cols = np.arange(64)
    cstart = np.clip(cols - 8, 0, 48)
    kc = np.arange(64)[:, None]
    inwin = (kc >= cstart[None, :]) & (kc < cstart[None, :] + 16)
    cm = np.where(inwin, 1.0, 0.0).astype(np.float32)
    cmask = np.ascontiguousarray(np.broadcast_to(np.concatenate([cm, cm], axis=0)[:, None, :], (128, 14, 64)))
    t = np.arange(S_)
    inv = np.empty((2, 128, S_), np.float32)
    for g, w in enumerate((2, 4, 8, 16)):
        lo = np.clip(t - w // 2, 0, S_ - 1)
        hi = np.clip(t + w // 2 - 1, 0, S_ - 1)
        inv[g // 2, (g % 2) * 64:(g % 2) * 64 + 64, :] = (1.0 / (hi - lo + 1).astype(np.float64)).astype(np.float32)[None, :]
    return ba, cmask, inv


class _Stop(Exception):
    pass


def build_nc(dbg=False, stop=None):
    nc = bass.Bass("TRN2", target_bir_lowering=False)
    dt_in = lambda n, s, d=F32: nc.dram_tensor(n, s, d, kind="ExternalInput").ap()
    x_in = dt_in("x", [S_, D_])
    colpack = dt_in("colpack", [128, NCOL])
    nfb_in = dt_in("nfb", [128, D_])
    w_ada = dt_in("w_ada", [2, D_, 6 * D_])
    w_in = dt_in("w_in", [2, D_, PW])
    w_pool = dt_in("w_pool", [2, 4, 64, 64])
    rpbt = dt_in("rpbt", [2, 6, 15, 64, 64])
    w_out = dt_in("w_out", [2, D_, D_])
    w_fi = dt_in("w_ffn_in", [2, D_, 2 * DFF])
    w_fo = dt_in("w_ffn_out", [2, DFF, D_])
    ba_in = dt_in("ba_tab", [128, 3, 6, 256])
    cm_in = dt_in("cmask", [128, 14, 64])
    inv_in = dt_in("invcnt", [2, 128, S_])
    y_out = nc.dram_tensor("y", [S_, D_], F32, kind="ExternalOutput").ap()
    kscr = "ExternalOutput" if dbg else "Internal"
    dscr = lambda n, s, d: nc.dram_tensor(n, s, d, kind=kscr).ap()
    qk_scr = dscr("qk_scr", [12, 128, S_], BF16)
    v_scr = dscr("v_scr", [2, S_, 384], BF16)
    u_scr = dscr("u_scr", [2, 128, S_], F32)
    xs1 = dscr("xs1", [S_, D_], F32)
    wi_scr = dscr("wi_scr", [2, NJ, 128, 2 * 8 * 128], BF16)
    wo_scr = dscr("wo_scr", [2, DFF, D_], BF16)
    wout_scr = dscr("wout_scr", [2, D_, D_], BF16)
    ec_scr = dscr("ec_scr", [2, 128, 6 * 14 * 64], BF16)
    mix_dbg = dscr("mix_dbg", [2, 8, 128, S_], BF16) if dbg else None

    with ExitStack() as es:
        S = Sched(nc, es)

        uid = [0]

        def sb(st, name, shape, dt):
            uid[0] += 1
            name = "%s_%d" % (name, uid[0])
            return TB(st.enter_context(nc.sbuf_tensor(name, shape, dt)), name)

        psall = es.enter_context(nc.psum_tensor("psall", [128, 8 * 512], F32))
        ps3 = psall[:, :].rearrange("p (b c) -> p b c", b=8)
        pb = [TB(psall[:, i * 512:(i + 1) * 512], "pb%d" % i) for i in range(8)]
        identf = sb(es, "identf", [128, 128], F32)
        onesf = sb(es, "onesf", [128, 128], F32)
        onesb = sb(es, "onesb", [128, 128], BF16)
        identb = sb(es, "identb", [128, 128], BF16)
        mixT = sb(es, "mixT", [128, 8, S_], BF16)
        cols = sb(es, "cols", [128, NCOL], F32)
        cact2 = sb(es, "cact2", [128, 8, 2], F32)
        modcol = sb(es, "modcol", [128, 48], F32)
        A1 = sb(es, "A1", [128, 8], F32)
        A2 = sb(es, "A2", [128, 8], F32)
        nfb = sb(es, "nfb_sb", [128, D_], F32)
        diag = [sb(es, "diag%d" % i, [128, 128], F32) for i in range(2)]
        ssq = sb(es, "ssq", [128, 4], F32)
        rs = sb(es, "rs", [128, 4], F32)
        junk = sb(es, "junk", [128, D_], BF16)
        qkb = [Buf("qk%d" % i) for i in range(12)]
        vb = [Buf("v%d" % i) for i in range(2)]
        ub = [Buf("u%d" % i) for i in range(2)]
        xs1b = Buf("xs1")
        yb = Buf("y")
        wib, wob, woutb = Buf("wi"), Buf("wo"), Buf("wout")
        ecb = Buf("ec")
        mixdbgb = Buf("mixdbg")

        S.op(S.pool, lambda: nc.gpsimd.memset(onesf.t[:, :], 1.0), writes=[onesf.b])
        S.op(S.pool, lambda: nc.gpsimd.memset(onesb.t[:, :], 1.0), writes=[onesb.b])
        S.op(S.pool, lambda: nc.gpsimd.affine_select(out=identf.t[:, :], in_=onesf.t[:, :], pattern=[[-1, 128]],
                                                     compare_op=ALU.is_equal, fill=0.0, base=0, channel_multiplier=1),
             reads=[onesf.b], writes=[identf.b])
        S.op(S.pool, lambda: nc.gpsimd.tensor_copy(out=identb.t[:, :], in_=identf.t[:, :]), reads=[identf.b], writes=[identb.b])
        S.dma(cols.t[:, :], colpack, writes=[cols.b])
        S.dma(nfb.t[:, :], nfb_in, writes=[nfb.b])
        for q in range(2):
            S.op(S.act, lambda: nc.scalar.activation(out=cact2.t[:, :, q], in_=cols.t[:, 0:8], func=AF.Silu),
                 reads=[cols.b], writes=[cact2.b], par=True)

        def rstd_from(ssq_ap, out_ap, n_feat, rb, wbuf):
            S.op(S.dve, lambda: nc.vector.tensor_scalar(out=out_ap, in0=ssq_ap, scalar1=1.0 / n_feat, scalar2=EPS,
                                                        op0=ALU.mult, op1=ALU.add), reads=rb, writes=[wbuf])
            S.op(S.act, lambda: nc.scalar.activation(out=out_ap, in_=out_ap, func=AF.Sqrt), reads=[wbuf], writes=[wbuf])
            S.op(S.dve, lambda: nc.vector.reciprocal(out=out_ap, in_=out_ap), reads=[wbuf], writes=[wbuf])

        evac_ctr = [0]

        def evac(out_ap, in_ap, reads, writes, scale=1.0, par=True):
            evac_ctr[0] += 1
            if evac_ctr[0] % 2 == 0:
                S.op(S.act, lambda: nc.scalar.activation(out=out_ap, in_=in_ap, func=AF.Copy, scale=float(scale)),
                     reads=reads, writes=writes, par=par)
            else:
                S.op(S.dve, lambda: nc.vector.tensor_scalar(out=out_ap, in0=in_ap, scalar1=float(scale), scalar2=None,
                                                            op0=ALU.mult), reads=reads, writes=writes, par=par)

        def ck(tag):
            if stop == tag:
                S.barrier()
                S.dead = True
        try:
          for l in range(2):
              cb = 8 + 72 * l
              x_src = x_in if l == 0 else xs1
              x_dst = xs1 if l == 0 else y_out
              xdb = xs1b if l == 0 else yb
              xsb = None if l == 0 else xs1b

              with ExitStack() as ls:
                  winb = sb(ls, "winb", [128, 8, PW], BF16)
                  g1b = sb(ls, "g1b", [128, D_], F32)
                  g2b = sb(ls, "g2b", [128, D_], F32)
                  with ExitStack() as ps_:
                      NST = 4
                      stage = [sb(ps_, "stage%d" % i, [128, 3 * D_], F32) for i in range(NST)]
                      obuf = [sb(ps_, "obuf%d" % i, [128, DFF], BF16) for i in range(NST)]
                      pmod = pb[6].t[:, 0:96].rearrange("p (j q) -> p j q", q=2)
                      modrow = sb(ps_, "modrow", [2, 3 * D_], F32)
                      ada_pieces = [(k, hf) for hf in range(2) for k in range(8)]

                      upieces = []
                      for i_ in range(len(ada_pieces)):
                          upieces.append(("ada",) + ada_pieces[i_])
                          if i_ % 2 == 1:
                              upieces.append(("win", i_ // 2))

                      def u_load(i):
                          pc_ = upieces[i]
                          stg = stage[i % NST]
                          if pc_[0] == "ada":
                              _, k, hf = pc_
                              S.dma(stg.t[:, :], w_ada[l, k * 128:(k + 1) * 128, hf * 3072:(hf + 1) * 3072], writes=[stg.b])
                          else:
                              S.dma(stg.t[:, 0:PW], w_in[l, pc_[1] * 128:(pc_[1] + 1) * 128, :], writes=[stg.b])
                      for i in range(3):
                          u_load(i)
                      for i, pc_ in enumerate(upieces):
                          stg = stage[i % NST]
                          if pc_[0] == "win":
                              kw_ = pc_[1]
                              S.op(S.act, lambda: nc.scalar.copy(out=winb.t[:, kw_, :], in_=stg.t[:, 0:PW]),
                                   reads=[stg.b], writes=[winb.b], par=True)
                          else:
                              _, k, hf = pc_

                              def f():
                                  for n_ in range(6):
                                      ins = nc.tensor.matmul(pb[n_].t[0:2, :], lhsT=cact2.t[:, k, :], rhs=stg.t[:, n_ * 512:(n_ + 1) * 512],
                                                             start=(k == 0), stop=(k == 7))
                                  return ins
                              S.op(S.pe, f, reads=[stg.b, cact2.b], writes=[pb[n_].b for n_ in range(6)])
                              if k == 7:
                                  for n_ in range(6):
                                      S.op(S.act, lambda: nc.scalar.copy(out=modrow.t[0:2, n_ * 512:(n_ + 1) * 512],
                                                                         in_=pb[n_].t[0:2, :]), reads=[pb[n_].b], writes=[modrow.b], par=(n_ > 0))

                                  def ftr():
                                      for jj in range(24):
                                          ins = nc.tensor.matmul(pmod[:, hf * 24 + jj, :], lhsT=modrow.t[0:1, jj * 128:(jj + 1) * 128],
                                                                 rhs=onesf.t[0:1, 0:2], start=True, stop=True, skip_group_check=True)
                                      return ins
                                  S.op(S.pe, ftr, reads=[modrow.b, onesf.b], writes=[pb[6].b])
                          if i + 3 < len(upieces):
                              u_load(i + 3)
                      S.op(S.dve, lambda: nc.vector.tensor_tensor(out=modcol.t[:, :], in0=pmod[:, :, 0],
                                                                  in1=cols.t[:, cb:cb + 48], op=ALU.add),
                           reads=[pb[6].b, cols.b], writes=[modcol.b])
                      S.op(S.dve, lambda: nc.vector.scalar_tensor_tensor(out=A1.t[:, :], in0=modcol.t[:, 8:16], scalar=1.0,
                                                                         in1=cols.t[:, cb + 48:cb + 56], op0=ALU.add, op1=ALU.mult),
                           reads=[modcol.b, cols.b], writes=[A1.b])
                      S.op(S.dve, lambda: nc.vector.scalar_tensor_tensor(out=A2.t[:, :], in0=modcol.t[:, 32:40], scalar=1.0,
                                                                         in1=cols.t[:, cb + 56:cb + 64], op0=ALU.add, op1=ALU.mult),
                           reads=[modcol.b, cols.b], writes=[A2.b])
                      for gi, (gb, gc0) in enumerate(((g1b, 16), (g2b, 40))):
                          for j in range(8):
                              dg = diag[j % 2]
                              S.op(S.dve, lambda: nc.vector.tensor_scalar(out=dg.t[:, :], in0=identf.t[:, :],
                                                                          scalar1=modcol.t[:, gc0 + j:gc0 + j + 1], scalar2=None,
                                                                          op0=ALU.mult), reads=[identf.b, modcol.b], writes=[dg.b])
                              bank = pb[1 + j // 4]
                              S.op(S.pe, lambda: nc.tensor.matmul(bank.t[:, (j % 4) * 128:(j % 4) * 128 + 128], lhsT=onesf.t[:, :],
                                                                  rhs=dg.t[:, :], start=True, stop=True, skip_group_check=True),
                                   reads=[dg.b, onesf.b], writes=[bank.b])
                          for hb in range(2):
                              S.op(S.act, lambda: nc.scalar.copy(out=gb.t[:, hb * 512:(hb + 1) * 512], in_=pb[1 + hb].t[:, :]),
                                   reads=[pb[1 + hb].b], writes=[gb.b], par=True)
                      pieces1 = [("win", k) for k in range(8)]
                      pieces2 = [("ec", h) for h in range(6)] + [("wout", k) for k in range(8)] + [("wfo", j) for j in range(NJ)] + \
                                [("wfi", k, g_, hf_) for k in range(8) for g_ in range(2) for hf_ in range(2)]

                      def w_load(pc_, stg):
                          if pc_[0] == "ec":
                              for j in range(2):
                                  S.dma(stg.t[64 * j:64 * j + 64, 0:896].rearrange("p (r c) -> p r c", c=64),
                                        rpbt[l, pc_[1], j:j + 14, :, :].rearrange("r k c -> k r c"), writes=[stg.b], par=(j > 0))
                          elif pc_[0] == "win":
                              S.dma(stg.t[:, 0:PW], w_in[l, pc_[1] * 128:(pc_[1] + 1) * 128, :], writes=[stg.b])
                          elif pc_[0] == "wout":
                              S.dma(stg.t[:, 0:D_], w_out[l, pc_[1] * 128:(pc_[1] + 1) * 128, :], writes=[stg.b])
                          elif pc_[0] == "wfo":
                              S.dma(stg.t[:, 0:D_], w_fo[l, pc_[1] * 128:(pc_[1] + 1) * 128, :], writes=[stg.b])
                          else:
                              c0_ = pc_[2] * DFF + pc_[3] * 1408
                              S.dma(stg.t[:, 0:1408], w_fi[l, pc_[1] * 128:(pc_[1] + 1) * 128, c0_:c0_ + 1408], writes=[stg.b])

                      def w_proc(pc_, stg, ob):
                          if pc_[0] == "ec":
                              h = pc_[1]
                              S.op(S.act, lambda: nc.scalar.activation(out=ob.t[:, 0:896], in_=stg.t[:, 0:896], func=AF.Exp), reads=[stg.b], writes=[ob.b])
                              S.dma(ec_scr[l, :, h * 896:(h + 1) * 896], ob.t[:, 0:896], reads=[ob.b], writes=[ecb], par=True, own=ob.b)
                          elif pc_[0] == "win":
                              k = pc_[1]
                              S.op(S.act, lambda: nc.scalar.copy(out=winb.t[:, k, :], in_=stg.t[:, 0:PW]),
                                   reads=[stg.b], writes=[winb.b], par=True)
                          elif pc_[0] == "wout":
                              k = pc_[1]
                              if k in (3, 4):
                                  S.op(S.dve, lambda: nc.vector.tensor_tensor(out=ob.t[:, 0:D_], in0=stg.t[:, 0:D_], in1=g1b.t[:, :],
                                                                              op=ALU.mult), reads=[stg.b, g1b.b], writes=[ob.b])
                              else:
                                  S.op(S.dve, lambda: nc.vector.scalar_tensor_tensor(out=ob.t[:, 0:D_], in0=stg.t[:, 0:D_],
                                                                                     scalar=cols.t[:, cb + 64 + k:cb + 65 + k], in1=g1b.t[:, :],
                                                                                     op0=ALU.mult, op1=ALU.mult),
                                       reads=[stg.b, g1b.b, cols.b], writes=[ob.b])
                              S.dma(wout_scr[l, k * 128:(k + 1) * 128, :], ob.t[:, 0:D_], reads=[ob.b], writes=[woutb], par=True, own=ob.b)
                          elif pc_[0] == "wfo":
                              j = pc_[1]
                              S.op(S.dve, lambda: nc.vector.tensor_tensor(out=ob.t[:, 0:D_], in0=stg.t[:, 0:D_], in1=g2b.t[:, :],
                                                                          op=ALU.mult), reads=[stg.b, g2b.b], writes=[ob.b])
                              S.dma(wo_scr[l, j * 128:(j + 1) * 128, :], ob.t[:, 0:D_], reads=[ob.b], writes=[wob], par=True, own=ob.b)
                          else:
                              k, g_, hf_ = pc_[1], pc_[2], pc_[3]
                              if g_ == 0:
                                  S.op(S.act, lambda: nc.scalar.copy(out=ob.t[:, 0:1408], in_=stg.t[:, 0:1408]), reads=[stg.b], writes=[ob.b])
                              else:
                                  S.op(S.dve, lambda: nc.vector.tensor_copy(out=ob.t[:, 0:1408], in_=stg.t[:, 0:1408]), reads=[stg.b], writes=[ob.b])
                              dst = wi_scr[l].rearrange("j p (g k c) -> p g k j c", g=2, k=8)[:, g_, k, hf_ * 11:(hf_ + 1) * 11, :]
                              S.dma(dst, ob.t[:, 0:1408].rearrange("p (j c) -> p j c", c=128), reads=[ob.b], writes=[wib], par=True, own=ob.b)
                      S.barrier()
                      ck('prep%d' % l)

                  with ExitStack() as p1:
                      xts = [sb(p1, "xt%d" % i, [128, 4, D_], F32) for i in range(2)]
                      hTs = [sb(p1, "hT%d" % i, [128, 8, 512], BF16) for i in range(2)]
                      qst = [sb(p1, "qst%d" % i, [128, 512], BF16) for i in range(4)]
                      vst = [sb(p1, "vst%d" % i, [128, 4, 384], BF16) for i in range(2)]
                      ust = [sb(p1, "ust%d" % i, [128, 512], F32) for i in range(2)]
                      stage2 = [sb(p1, "stage2_%d" % i, [128, 1408], F32) for i in range(3)]
                      obuf2 = [sb(p1, "obuf2_%d" % i, [128, 1408], BF16) for i in range(2)]

                      def prep2_gen():
                          n2 = len(pieces2)
                          w_load(pieces2[0], stage2[0])
                          w_load(pieces2[1], stage2[1])
                          yield
                          for i2 in range(n2):
                              if i2 + 2 < n2:
                                  w_load(pieces2[i2 + 2], stage2[(i2 + 2) % 3])
                              w_proc(pieces2[i2], stage2[i2 % 3], obuf2[i2 % 2])
                              yield
                      pg2 = prep2_gen()
                      qk_cols = [0, 128, 256, 384, 512, 640, 1408, 1536, 1664, 1792, 1920, 2048]
                      is_q = [1, 1, 1, 0, 0, 0, 1, 1, 1, 0, 0, 0]
                      bctr = [0]

                      def nbank():
                          bctr[0] += 1
                          return pb[2 + bctr[0] % 6]
                      def xload(tt_):
                          S.dma(xts[tt_ % 2].t[:, :, :], x_src[tt_ * 512:(tt_ + 1) * 512, :].rearrange("(s p) d -> p s d", p=128),
                                reads=([xsb] if xsb else []), writes=[xts[tt_ % 2].b])
                      def P1_norm(tt):
                          xt = xts[tt % 2]
                          hT = hTs[tt % 2]
                          for s in range(4):
                              S.op(S.act, lambda: nc.scalar.activation(out=junk.t[:, :], in_=xt.t[:, s, :], func=AF.Square,
                                                                       accum_out=ssq.t[:, s:s + 1]),
                                   reads=[xt.b], writes=[junk.b, ssq.b])
                          rstd_from(ssq.t[:, :], rs.t[:, :], D_, [ssq.b], rs.b)
                          for s in range(4):
                              S.op(S.dve, lambda: nc.vector.tensor_scalar(out=xt.t[:, s, :], in0=xt.t[:, s, :],
                                                                          scalar1=rs.t[:, s:s + 1], scalar2=None, op0=ALU.mult),
                                   reads=[rs.b, xt.b], writes=[xt.b], par=True)
                      def P1_norm_gen(tt):
                          xt = xts[tt % 2]
                          for s in range(4):
                              S.op(S.act, lambda: nc.scalar.activation(out=junk.t[:, :], in_=xt.t[:, s, :], func=AF.Square,
                                                                       accum_out=ssq.t[:, s:s + 1]),
                                   reads=[xt.b], writes=[junk.b, ssq.b])
                              yield
                          rstd_from(ssq.t[:, :], rs.t[:, :], D_, [ssq.b], rs.b)
                          yield
                          for s in range(4):
                              S.op(S.dve, lambda: nc.vector.tensor_scalar(out=xt.t[:, s, :], in0=xt.t[:, s, :],
                                                                          scalar1=rs.t[:, s:s + 1], scalar2=None, op0=ALU.mult),
                                   reads=[rs.b, xt.b], writes=[xt.b], par=True)
                              yield
                      ngen = [None]

                      def P1_tr(tt):
                          xt = xts[tt % 2]
                          hT = hTs[tt % 2]
                          for c in range(8):
                              pt = pb[c % 2]

                              def f():
                                  for s in range(4):
                                      i = nc.tensor.transpose(out=pt.t[:, s * 128:(s + 1) * 128],
                                                              in_=xt.t[:, s, c * 128:(c + 1) * 128], identity=identf.t[:, :])
                                  return i
                              S.op(S.pe, f, reads=[xt.b, identf.b], writes=[pt.b])
                              S.op(S.act, lambda: nc.scalar.activation(out=hT.t[:, c, :], in_=pt.t[:, :], func=AF.Identity,
                                                                       bias=modcol.t[:, c:c + 1], scale=A1.t[:, c:c + 1]),
                                   reads=[pt.b, modcol.b, A1.b], writes=[hT.b], par=True)
                      def P1_qk(tt):
                          xt = xts[tt % 2]
                          hT = hTs[tt % 2]
                          for ci, col in enumerate(qk_cols):
                              pp = nbank()

                              def f():
                                  for k in range(8):
                                      i = nc.tensor.matmul(pp.t[:, :], lhsT=winb.t[:, k, col:col + 128], rhs=hT.t[:, k, :],
                                                           start=(k == 0), stop=(k == 7))
                                  return i
                              S.op(S.pe, f, reads=[winb.b, hT.b], writes=[pp.b])
                              st = qst[ci % 4]
                              evac(st.t[:, :], pp.t[:, :], [pp.b], [st.b], scale=(0.125 if is_q[ci] else 1.0), par=False)
                              S.dma(qk_scr[ci, :, tt * 512:(tt + 1) * 512], st.t[:, :], reads=[st.b], writes=[qkb[ci]], par=True, own=st.b)
                              if ci % 2 == 1:
                                  next(pg2, None)
                              if ngen[0] is not None and ci >= 2:
                                  next(ngen[0], None)
                      def P1_v(tt):
                          xt = xts[tt % 2]
                          hT = hTs[tt % 2]
                          for mi, vcol in enumerate((768, 2176)):
                              vs = vst[mi]
                              for s in range(4):
                                  pp = nbank()

                                  def f():
                                      for k in range(8):
                                          i = nc.tensor.matmul(pp.t[:, 0:384], lhsT=hT.t[:, k, s * 128:(s + 1) * 128],
                                                               rhs=winb.t[:, k, vcol:vcol + 384], start=(k == 0), stop=(k == 7))
                                      return i
                                  S.op(S.pe, f, reads=[winb.b, hT.b], writes=[pp.b])
                                  evac(vs.t[:, s, :], pp.t[:, 0:384], [pp.b], [vs.b], par=True)
                                  if s % 2 == 1:
                                      next(pg2, None)
                              S.dma(v_scr[mi, tt * 512:(tt + 1) * 512, :].rearrange("(s p) f -> p s f", p=128), vs.t[:, :, :],
                                    reads=[vs.b], writes=[vb[mi]], par=True, own=vs.b)
                          for ui, ucol in enumerate((1152, 1280)):
                              pp = nbank()

                              def f():
                                  for k in range(8):
                                      i = nc.tensor.matmul(pp.t[:, :], lhsT=winb.t[:, k, ucol:ucol + 128], rhs=hT.t[:, k, :],
                                                           start=(k == 0), stop=(k == 7))
                                  return i
                              S.op(S.pe, f, reads=[winb.b, hT.b], writes=[pp.b])
                              evac(ust[ui].t[:, :], pp.t[:, :], [pp.b], [ust[ui].b], par=False)
                              S.dma(u_scr[ui, :, tt * 512:(tt + 1) * 512], ust[ui].t[:, :], reads=[ust[ui].b], writes=[ub[ui]], par=True, own=ust[ui].b)
                      xload(0)
                      xload(1)
                      P1_norm(0)
                      P1_tr(0)
                      for tt in range(8):
                          ngen[0] = P1_norm_gen(tt + 1) if tt + 1 < 8 else None
                          P1_qk(tt)
                          if ngen[0] is not None:
                              for _ in ngen[0]:
                                  pass
                          P1_v(tt)
                          if tt + 1 < 8:
                              P1_tr(tt + 1)
                          if tt + 2 < 8:
                              xload(tt + 2)
                      for _ in pg2:
                          pass
                      S.barrier()
                      ck('p1_%d' % l)

              with ExitStack() as pa:
                  QT = sb(pa, "QT", [128, S_], BF16)
                  KT = sb(pa, "KT", [128, S_], BF16)
                  Vd = [sb(pa, "Vd%d" % i, [128, 32, 128], BF16) for i in range(3)]
                  QTd = [None, sb(pa, "QT4", [128, S_], BF16), sb(pa, "QT16", [128, S_], BF16)]
                  KTd = [None, sb(pa, "KT4", [128, S_], BF16), sb(pa, "KT16", [128, S_], BF16)]
                  estg = [sb(pa, "estg%d" % i, [128, 512], F32) for i in range(2)]
                  ectr = [0]
                  rfin = [sb(pa, "rfin%d" % i, [128, 512], F32) for i in range(4)]
                  accO = sb(pa, "accO", [128, S_], F32)
                  accD = sb(pa, "accD", [128, S_], F32)
                  BA = sb(pa, "BA", [128, 3, 6, 256], BF16)
                  sbias = [sb(pa, "sbias%d" % i, [128, 2, 256], BF16) for i in range(4)]
                  S.dma(BA.t[:, :, :, :], ba_in, writes=[BA.b], E=S.pool)
                  PT = [sb(pa, "PT%d" % i, [128, 2, 256], BF16) for i in range(4)]
                  kctr = 0
                  def a_loads(p_, which):
                      if "qk" in which:
                          S.dma(QT.t[:, :], qk_scr[p_], reads=[qkb[p_]], writes=[QT.b])
                          S.dma(KT.t[:, :], qk_scr[3 + p_], reads=[qkb[3 + p_]], writes=[KT.b])
                      for bi, d in enumerate((1, 4, 16)):
                          if ("v%d" % bi) not in which:
                              continue
                          src = v_scr[0][:, p_ * 128:(p_ + 1) * 128].rearrange("(jb q r) f -> q r jb f", r=d, q=128)
                          for r in range(d):
                              nkb_ = 32 // d
                              S.dma(Vd[bi].t[:, r * nkb_:(r + 1) * nkb_, :], src[:, r, :, :], reads=[vb[0]], writes=[Vd[bi].b],
                                    par=(r > 0))
                  a_loads(0, ("qk", "v0", "v1", "v2"))
                  for p in range(3):
                      if p > 0:
                          a_loads(p, ("v2",))
                      QTd[0], KTd[0] = QT, KT

                      def deint(bi, d, c):
                          w_ = 512 // d
                          for (src_, dst_) in ((QT, QTd[bi]), (KT, KTd[bi])):
                              S.op(S.act, lambda: nc.scalar.copy(out=dst_.t[:, :].rearrange("p (r i) -> p r i", r=d)[:, :, c * w_:(c + 1) * w_],
                                                                 in_=src_.t[:, c * 512:(c + 1) * 512].rearrange("p (i r) -> p r i", r=d)),
                                   reads=[src_.b], writes=[dst_.b], par=(c > 0))

                      def norm_chunk(pp_, c):
                          ts_ = slice(c * 512, (c + 1) * 512)
                          ra, rb_ = rfin[c % 2], rfin[2 + c % 2]
                          S.op(S.act, lambda: nc.scalar.activation(out=ra.t[:, :], in_=accD.t[:, ts_], func=AF.Ln), reads=[accD.b], writes=[ra.b])
                          S.op(S.act, lambda: nc.scalar.activation(out=rb_.t[:, :], in_=ra.t[:, :], func=AF.Exp, scale=-1.0),
                               reads=[ra.b], writes=[rb_.b])
                          S.op(S.dve, lambda: nc.vector.scalar_tensor_tensor(out=ra.t[:, :], in0=accD.t[:, ts_], scalar=-1.0, in1=rb_.t[:, :],
                                                                             op0=ALU.mult, op1=ALU.mult),
                               reads=[accD.b, rb_.b], writes=[ra.b])
                          S.op(S.dve, lambda: nc.vector.scalar_tensor_tensor(out=ra.t[:, :], in0=ra.t[:, :], scalar=2.0, in1=rb_.t[:, :],
                                                                             op0=ALU.add, op1=ALU.mult),
                               reads=[ra.b, rb_.b], writes=[ra.b])
                          S.op(S.dve, lambda: nc.vector.tensor_tensor(out=mixT.t[:, pp_, ts_], in0=accO.t[:, ts_], in1=ra.t[:, :], op=ALU.mult),
                               reads=[accO.b, ra.b], writes=[mixT.b], par=True)
                      tasks = []
                      for bi, d in enumerate((1, 4, 16)):
                          for r in range(d):
                              for jb in range((S_ // d) // 128):
                                  tasks.append((bi, d, r, jb))
                      started = {}
                      segbank = {}

                      def stA(task):
                          nonlocal kctr
                          bi, d, r, jb = task
                          n = S_ // d
                          q0 = max(0, jb * 128 - 64)
                          q1 = min(n, jb * 128 + 192)
                          c0 = q0 - (jb * 128 - 64)
                          c1 = c0 + (q1 - q0)
                          kctr += 1
                          b0 = 2 * (kctr % 3)
                          spbs = [pb[b0], pb[b0 + 1]]
                          pt_ = PT[kctr % 4]

                          def tsl(a, b):
                              return slice(r + d * a, r + d * (b - 1) + 1, d)
                          qt_, kt_ = QTd[bi], KTd[bi]
                          ro = r * n

                          def f():
                              for hh in range(2):
                                  i = nc.tensor.matmul(spbs[hh].t[:, c0:c1],
                                                       lhsT=kt_.t[hh * 64:(hh + 1) * 64, ro + jb * 128:ro + jb * 128 + 128],
                                                       rhs=qt_.t[hh * 64:(hh + 1) * 64, ro + q0:ro + q1],
                                                       start=True, stop=True, skip_group_check=True)
                              return i
                          S.op(S.pe, f, reads=[kt_.b, qt_.b], writes=[spbs[0].b, spbs[1].b])
                          sbs = sbias[kctr % 4]
                          S.op(S.act, lambda: nc.scalar.activation(out=sbs.t[:, :, c0:c1], in_=ps3[:, b0:b0 + 2, c0:c1], func=AF.Exp),
                               reads=[spbs[0].b, spbs[1].b], writes=[sbs.b])
                          S.op(S.dve, lambda: nc.vector.tensor_tensor(out=pt_.t[:, :, c0:c1], in0=sbs.t[:, :, c0:c1],
                                                                      in1=BA.t[:, bi, 2 * p:2 * p + 2, c0:c1], op=ALU.mult),
                               reads=[sbs.b, BA.b], writes=[pt_.b])
                          return (task, q0, q1, c0, pt_, tsl)

                      def stB(ctx):
                          (bi, d, r, jb), q0, q1, c0, pt_, tsl = ctx
                          n = S_ // d
                          nkb = n // 128
                          seg = 256
                          nseg = n // seg
                          kps = seg // 128
                          st_ = started.setdefault((bi, r), set())

                          def sbank(sg):
                              key = (bi, r, sg)
                              if key not in segbank:
                                  segbank[key] = pb[6 + len(segbank) % 2]
                              return segbank[key]
                          pieces = []
                          qa = q0
                          while qa < q1:
                              sg = qa // seg
                              qb_ = min(q1, (sg + 1) * seg)
                              pieces.append((sg, qa, qb_))
                              qa = qb_
                          banks = []
                          for (sg, _, _) in pieces:
                              banks += [sbank(sg).b]

                          def g():
                              for (sg, qa, qb_) in pieces:
                                  jlast = min(nkb - 1, (sg + 1) * kps)
                                  for hh in range(2):
                                      first = (sg, hh) not in st_
                                      st_.add((sg, hh))
                                      last = (jb == jlast)
                                      rhs = pt_.t[:, hh, c0 + (qa - q0):c0 + (qb_ - q0)]
                                      nc.tensor.matmul(sbank(sg).t[hh * 64:(hh + 1) * 64, qa - sg * seg:qb_ - sg * seg],
                                                       lhsT=Vd[bi].t[:, r * nkb + jb, hh * 64:(hh + 1) * 64], rhs=rhs,
                                                       start=first, stop=False, skip_group_check=True)
                                      i = nc.tensor.matmul(sbank(sg).t[hh * 64:(hh + 1) * 64, 256 + qa - sg * seg:256 + qb_ - sg * seg],
                                                           lhsT=onesb.t[:, 0:64], rhs=rhs,
                                                           start=False, stop=last, skip_group_check=True)
                              return i
                          S.op(S.pe, g, reads=[pt_.b, Vd[bi].b, onesb.b], writes=banks)
                          for sg in range(nseg):
                              if jb == min(nkb - 1, (sg + 1) * kps):
                                  ts_ = tsl(sg * seg, (sg + 1) * seg)
                                  ob_ = sbank(sg)
                                  if bi == 0:
                                      S.op(S.act, lambda: nc.scalar.copy(out=accO.t[:, ts_], in_=ob_.t[:, 0:seg]),
                                           reads=[ob_.b], writes=[accO.b], par=True)
                                      S.op(S.act, lambda: nc.scalar.copy(out=accD.t[:, ts_], in_=ob_.t[:, 256:256 + seg]),
                                           reads=[ob_.b], writes=[accD.b], par=True)
                                  else:
                                      ectr[0] += 1
                                      es_ = estg[ectr[0] % 2]
                                      S.op(S.act, lambda: nc.scalar.copy(out=es_.t[:, :], in_=ob_.t[:, :]), reads=[ob_.b], writes=[es_.b])
                                      S.op(S.pool, lambda: nc.gpsimd.tensor_tensor(out=accO.t[:, ts_], in0=es_.t[:, 0:seg],
                                                                                   in1=accO.t[:, ts_], op=ALU.add),
                                           reads=[es_.b, accO.b], writes=[accO.b], par=True)
                                      S.op(S.pool, lambda: nc.gpsimd.tensor_tensor(out=accD.t[:, ts_], in0=es_.t[:, 256:256 + seg],
                                                                                   in1=accD.t[:, ts_], op=ALU.add),
                                           reads=[es_.b, accD.b], writes=[accD.b], par=True)

                      ctxs = [stA(tasks[0]), stA(tasks[1])]
                      for ti in range(2, len(tasks)):
                          if p > 0 and ti % 2 == 0 and 2 <= ti <= 16:
                              norm_chunk(p - 1, (ti - 2) // 2)
                          if ti % 2 == 1 and 3 <= ti <= 17:
                              deint(1, 4, (ti - 3) // 2)
                          if ti % 2 == 1 and 33 <= ti <= 47:
                              deint(2, 16, (ti - 33) // 2)
                          if ti == 50 and p + 1 < 3:
                              a_loads(p + 1, ("qk", "v0"))
                          if ti == 72 and p + 1 < 3:
                              a_loads(p + 1, ("v1",))
                          ctxs.append(stA(tasks[ti]))
                          stB(ctxs[ti - 2])
                      stB(ctxs[-2])
                      stB(ctxs[-1])
                      if p == 2:
                          for c_ in range(8):
                              ts_ = slice(c_ * 512, (c_ + 1) * 512)
                              ra, rb_ = rfin[c_ % 2], rfin[2 + c_ % 2]
                              S.op(S.act, lambda: nc.scalar.activation(out=ra.t[:, :], in_=accD.t[:, ts_], func=AF.Ln), reads=[accD.b], writes=[ra.b])
                              S.op(S.act, lambda: nc.scalar.activation(out=rb_.t[:, :], in_=ra.t[:, :], func=AF.Exp, scale=-1.0),
                                   reads=[ra.b], writes=[rb_.b])
                              S.op(S.dve, lambda: nc.vector.scalar_tensor_tensor(out=ra.t[:, :], in0=accD.t[:, ts_], scalar=-1.0, in1=rb_.t[:, :],
                                                                                 op0=ALU.mult, op1=ALU.mult),
                                   reads=[accD.b, rb_.b], writes=[ra.b])
                              S.op(S.dve, lambda: nc.vector.scalar_tensor_tensor(out=ra.t[:, :], in0=ra.t[:, :], scalar=2.0, in1=rb_.t[:, :],
                                                                                 op0=ALU.add, op1=ALU.mult),
                                   reads=[ra.b, rb_.b], writes=[ra.b])
                              S.op(S.dve, lambda: nc.vector.tensor_tensor(out=mixT.t[:, p, ts_], in0=accO.t[:, ts_], in1=ra.t[:, :], op=ALU.mult),
                                   reads=[accO.b, ra.b], writes=[mixT.b], par=True)
                  S.barrier()
                  ck('p2a_%d' % l)


              def pool_gen(U, T1, T2, invc, pd, wpf, wpb):
                  PAD = 16
                  L = S_ + 2 * PAD
                  for tz in (U, T1, T2):
                      S.op(S.pool, lambda: nc.gpsimd.memset(tz.t[:, :], 0.0), writes=[tz.b])
                  yield
                  for ch in range(2):
                      S.dma(U.t[:, PAD:PAD + S_], u_scr[ch], reads=[ub[ch]], writes=[U.b])
                      S.dma(invc.t[:, :], inv_in[ch], writes=[invc.b])
                      S.op(S.pool, lambda: nc.gpsimd.memset(wpf.t[:, :], 0.0), writes=[wpf.b])
                      for g2_ in range(2):
                          S.dma(wpf.t[g2_ * 64:(g2_ + 1) * 64, g2_ * 64:(g2_ + 1) * 64], w_pool[l, ch * 2 + g2_], writes=[wpf.b])
                      yield
                      S.op(S.act, lambda: nc.scalar.copy(out=wpb.t[:, :], in_=wpf.t[:, :]), reads=[wpf.b], writes=[wpb.b])
                      yield

                      def shadd(dst, src, sh, prt):
                          S.op(S.dve, lambda: nc.vector.tensor_tensor(out=dst.t[prt, 0:L - sh], in0=src.t[prt, 0:L - sh],
                                                                      in1=src.t[prt, sh:L], op=ALU.add),
                               reads=[src.b], writes=[dst.b])
                      allp = slice(0, 128)
                      hi_ = slice(64, 128)
                      if ch == 0:
                          shadd(T1, U, 1, allp)
                          yield
                          shadd(T2, T1, 2, hi_)
                          yield
                          fin = ((T1, slice(0, 64), 1), (T2, hi_, 2))
                      else:
                          shadd(T1, U, 1, allp)
                          yield
                          shadd(T2, T1, 2, allp)
                          yield
                          shadd(T1, T2, 4, allp)
                          yield
                          shadd(T2, T1, 8, hi_)
                          yield
                          fin = ((T1, slice(0, 64), 4), (T2, hi_, 8))
                      for (src, prt, hw) in fin:
                          S.op(S.dve, lambda: nc.vector.tensor_tensor(out=invc.t[prt, :], in0=src.t[prt, PAD - hw:PAD - hw + S_],
                                                                      in1=invc.t[prt, :], op=ALU.mult),
                               reads=[src.b, invc.b], writes=[invc.b], par=True)
                          yield
                      S.op(S.dve, lambda: nc.vector.tensor_tensor(out=pd.t[:, :], in0=invc.t[:, :], in1=U.t[:, PAD:PAD + S_],
                                                                  op=ALU.subtract), reads=[invc.b, U.b], writes=[pd.b])
                      yield
                      for tt in range(8):
                          pp = pb[4 + tt % 2]
                          S.op(S.pe, lambda: nc.tensor.matmul(pp.t[:, :], lhsT=wpb.t[:, :], rhs=pd.t[:, tt * 512:(tt + 1) * 512],
                                                              start=True, stop=True, skip_group_check=True), reads=[wpb.b, pd.b], writes=[pp.b])
                          S.op(S.act, lambda: nc.scalar.activation(out=mixT.t[:, 3 + ch, tt * 512:(tt + 1) * 512], in_=pp.t[:, :],
                                                                   func=AF.Identity, scale=cols.t[:, cb + 64 + 3 + ch:cb + 64 + 4 + ch]),
                               reads=[pp.b, cols.b], writes=[mixT.b], par=True)
                          yield

              with ExitStack() as pc:
                  QT = sb(pc, "QTc", [128, S_], BF16)
                  KT = sb(pc, "KTc", [128, S_], BF16)
                  Ve = sb(pc, "Ve", [128, 32, 128], BF16)
                  Vo = sb(pc, "Vo", [128, 32, 128], BF16)
                  sbias = [sb(pc, "sbc%d" % i, [128, 512], BF16) for i in range(4)]
                  EC = sb(pc, "EC", [128, 6, 14, 64], BF16)
                  PT = [sb(pc, "PTc%d" % i, [128, 512], BF16) for i in range(4)]
                  rden = [sb(pc, "rden%d" % i, [128, 256], F32) for i in range(2)]
                  rtmp_c = [sb(pc, "rtmpc%d" % i, [128, 256], F32) for i in range(2)]
                  def c_loads(p_):
                      S.dma(QT.t[:, :], qk_scr[6 + p_], reads=[qkb[6 + p_]], writes=[QT.b])
                      S.dma(KT.t[:, :], qk_scr[9 + p_], reads=[qkb[9 + p_]], writes=[KT.b])
                      vsrc = v_scr[1][:, p_ * 128:(p_ + 1) * 128]
                      S.dma(Ve.t[:, :, :], vsrc.rearrange("(t q) f -> q t f", q=128), reads=[vb[1]], writes=[Ve.b])
                      S.dma(Vo.t[:, 0:31, :], vsrc[64:64 + 31 * 128, :].rearrange("(t q) f -> q t f", q=128), reads=[vb[1]], writes=[Vo.b])
                  c_loads(0)
                  cmk = sb(pc, "cmk", [128, 14, 64], F32)
                  S.dma(cmk.t[:, :, :], cm_in, writes=[cmk.b])
                  S.dma(EC.t[:, :, :, :], ec_scr[l].rearrange("p (h r c) -> p h r c", h=6, r=14), reads=[ecb], writes=[EC.b])
                  for h in range(6):
                      S.op(S.dve, lambda: nc.vector.tensor_tensor(out=EC.t[:, h, :, :], in0=EC.t[:, h, :, :], in1=cmk.t[:, :, :], op=ALU.mult),
                           reads=[EC.b, cmk.b], writes=[EC.b], par=True)
                  PADp = 16
                  ptiles = (sb(pc, "U", [128, S_ + 2 * PADp], F32), sb(pc, "T1", [128, S_ + 2 * PADp], F32),
                            sb(pc, "T2", [128, S_ + 2 * PADp], F32), sb(pc, "invc", [128, S_], F32), sb(pc, "pd", [128, S_], BF16),
                            sb(pc, "wpf", [128, 128], F32), sb(pc, "wpb", [128, 128], BF16))
                  pgen = pool_gen(*ptiles)
                  kctr = 0
                  for p in range(3):
                      if p > 0:
                          c_loads(p)
                      def cA(r):
                          nonlocal kctr
                          rs_ = min(max(r - 4, 0), 56)
                          dr0 = rs_ - r + 7
                          kctr += 1
                          b0 = 2 * (kctr % 3)
                          spbs = [pb[b0], pb[b0 + 1]]
                          pt_ = PT[kctr % 4]

                          def f():
                              for hh in range(2):
                                  for i4 in range(4):
                                      a = rs_ + 2 * i4
                                      i = nc.tensor.matmul(spbs[hh].t[:, i4 * 64:i4 * 64 + 64],
                                                           lhsT=KT.t[hh * 64:(hh + 1) * 64, a * 64:a * 64 + 128],
                                                           rhs=QT.t[hh * 64:(hh + 1) * 64, r * 64:r * 64 + 64],
                                                           start=True, stop=True, skip_group_check=True)
                              return i
                          S.op(S.pe, f, reads=[KT.b, QT.b], writes=[spbs[0].b, spbs[1].b])
                          sbs = sbias[kctr % 4]
                          S.op(S.act, lambda: nc.scalar.activation(out=sbs.t[:, :].rearrange("p (h c) -> p h c", h=2),
                                                                   in_=ps3[:, b0:b0 + 2, 0:256], func=AF.Exp),
                               reads=[spbs[0].b, spbs[1].b], writes=[sbs.b])
                          S.op(S.dve, lambda: nc.vector.tensor_tensor(
                              out=pt_.t[:, :].rearrange("p (h i c) -> p h i c", h=2, i=4),
                              in0=sbs.t[:, :].rearrange("p (h i c) -> p h i c", h=2, i=4),
                              in1=EC.t[:, 2 * p:2 * p + 2, dr0:dr0 + 7:2, :], op=ALU.mult),
                              reads=[sbs.b, EC.b], writes=[pt_.b])
                          return (r, rs_, pt_)

                      def cB(ctx):
                          r, rs_, pt_ = ctx
                          ob_ = pb[6 + (r // 4) % 2]

                          def g():
                              for hh in range(2):
                                  for i4 in range(4):
                                      a = rs_ + 2 * i4
                                      vt = Ve.t[:, a // 2, hh * 64:(hh + 1) * 64] if a % 2 == 0 else Vo.t[:, (a - 1) // 2, hh * 64:(hh + 1) * 64]
                                      rhs = pt_.t[:, hh * 256 + i4 * 64:hh * 256 + i4 * 64 + 64]
                                      nc.tensor.matmul(ob_.t[hh * 64:(hh + 1) * 64, (r % 4) * 64:(r % 4) * 64 + 64], lhsT=vt, rhs=rhs,
                                                       start=(i4 == 0), stop=False, skip_group_check=True)
                                      i = nc.tensor.matmul(ob_.t[hh * 64:(hh + 1) * 64, 256 + (r % 4) * 64:256 + (r % 4) * 64 + 64],
                                                           lhsT=onesb.t[:, 0:64], rhs=rhs,
                                                           start=False, stop=(i4 == 3), skip_group_check=True)
                              return i
                          S.op(S.pe, g, reads=[pt_.b, Ve.b, Vo.b, onesb.b], writes=[ob_.b])
                          if r % 4 == 3:
                              rd = rden[(r // 4) % 2]
                              tn = rtmp_c[(r // 4) % 2]
                              den_ = ob_.t[:, 256:512]
                              S.op(S.act, lambda: nc.scalar.activation(out=tn.t[:, :], in_=den_, func=AF.Ln), reads=[ob_.b], writes=[tn.b])
                              S.op(S.act, lambda: nc.scalar.activation(out=rd.t[:, :], in_=tn.t[:, :], func=AF.Exp, scale=-1.0),
                                   reads=[tn.b], writes=[rd.b])
                              S.op(S.dve, lambda: nc.vector.scalar_tensor_tensor(out=tn.t[:, :], in0=den_, scalar=-1.0, in1=rd.t[:, :],
                                                                                 op0=ALU.mult, op1=ALU.mult),
                                   reads=[ob_.b, rd.b], writes=[tn.b])
                              S.op(S.dve, lambda: nc.vector.scalar_tensor_tensor(out=tn.t[:, :], in0=tn.t[:, :], scalar=2.0, in1=rd.t[:, :],
                                                                                 op0=ALU.add, op1=ALU.mult),
                                   reads=[tn.b, rd.b], writes=[tn.b])
                              S.op(S.dve, lambda: nc.vector.tensor_tensor(out=mixT.t[:, 5 + p, (r - 3) * 64:(r + 1) * 64],
                                                                          in0=ob_.t[:, 0:256], in1=tn.t[:, :], op=ALU.mult),
                                   reads=[ob_.b, tn.b], writes=[mixT.b], par=True)

                      cctx = [cA(0), cA(1)]
                      for r in range(2, 64):
                          cctx.append(cA(r))
                          cB(cctx[r - 2])
                          if r % 3 == 0:
                              next(pgen, None)
                      cB(cctx[-2])
                      cB(cctx[-1])
                  for _ in pgen:
                      pass
                  S.barrier()
                  ck('p2c_%d' % l)

              with ExitStack() as pn:
                  if dbg:
                      for c in range(8):
                          S.dma(mix_dbg[l, c], mixT.t[:, c, :], reads=[mixT.b], writes=[mixdbgb], par=True, own=mixT.b)
                  S.barrier()
                  ck('norm_%d' % l)

              with ExitStack() as p3:
                  woutb_sb = sb(p3, "wout_sb", [128, 8, D_], BF16)
                  xt3s = [sb(p3, "xt3_%d" % i, [128, 4, D_], F32) for i in range(2)]
                  xn3 = sb(p3, "xn3", [128, 4, D_], F32)
                  hT3s = [sb(p3, "hT3_%d" % i, [128, 8, 512], BF16) for i in range(2)]
                  actb = sb(p3, "actb", [128, NJ, 512], BF16)
                  wis = [sb(p3, "wis%d" % i, [128, 2, 8, 128], BF16) for i in range(3)]
                  wos = [sb(p3, "wos%d" % i, [128, D_], BF16) for i in range(3)]
                  sil = [sb(p3, "sil%d" % i, [128, 512], F32) for i in range(2)]
                  osq = [sb(p3, "osq%d" % i, [128, 3, 512], BF16) for i in range(2)]
                  rsm = sb(p3, "rsm", [128, 8], F32)
                  S.dma(woutb_sb.t[:, :, :], wout_scr[l].rearrange("(k p) n -> p k n", p=128), reads=[woutb], writes=[woutb_sb.b])
                  wlist = []
                  for tt_ in range(8):
                      wlist += [("wi", j_) for j_ in range(NJ)] + [("wo", j_) for j_ in range(NJ)]
                  wstate = [0]

                  def wneed(i):
                      while wstate[0] < len(wlist) and wstate[0] <= i + 2:
                          n_ = wstate[0]
                          kind, j_ = wlist[n_]
                          if kind == "wi":
                              wi_ = wis[n_ % 3]
                              S.dma(wi_.t[:, :, :, :], wi_scr[l, j_].rearrange("p (g k c) -> p g k c", g=2, k=8), reads=[wib], writes=[wi_.b])
                          else:
                              wo_ = wos[n_ % 3]
                              S.dma(wo_.t[:, :], wo_scr[l, j_ * 128:(j_ + 1) * 128, :], reads=[wob], writes=[wo_.b])
                          wstate[0] += 1

                  def xload3(tt_):
                      S.dma(xt3s[tt_ % 2].t[:, :, :], x_src[tt_ * 512:(tt_ + 1) * 512, :].rearrange("(s p) d -> p s d", p=128),
                            reads=([xsb] if xsb else []), writes=[xt3s[tt_ % 2].b])
                  xload3(0)
                  wi_idx = 0
                  def X_a(tt):
                      nonlocal wi_idx
                      tsl_ = slice(tt * 512, (tt + 1) * 512)
                      xt = xt3s[tt % 2]
                      for mi_, c0_ in enumerate((0, 5)):
                          oq = osq[mi_]
                          S.op(S.act, lambda: nc.scalar.activation(out=oq.t[:, :, :], in_=mixT.t[:, c0_:c0_ + 3, tsl_], func=AF.Square),
                               reads=[mixT.b], writes=[oq.b])

                          def f():
                              for s in range(4):
                                  for c in range(3):
                                      i = nc.tensor.matmul(pb[6].t[:, (mi_ * 4 + s) * 2:(mi_ * 4 + s) * 2 + 2],
                                                           lhsT=oq.t[:, c, s * 128:(s + 1) * 128], rhs=onesb.t[:, 0:2],
                                                           start=(c == 0), stop=(c == 2), skip_group_check=True)
                              return i
                          S.op(S.pe, f, reads=[oq.b, onesb.b], writes=[pb[6].b])
                      rstd_from(pb[6].t[:, 0:16].rearrange("p (a q) -> p a q", q=2)[:, :, 0], rsm.t[:, :], 384.0, [pb[6].b], rsm.b)
                      for s in range(4):
                          for nh in range(2):
                              bb = ((s * 2 + nh) % 2) * 3
                              pA, pB, pC = pb[bb], pb[bb + 1], pb[bb + 2]

                              def f():
                                  for (pq, ks) in ((pA, (0, 1, 2)), (pB, (3, 4)), (pC, (5, 6, 7))):
                                      for k in ks:
                                          i = nc.tensor.matmul(pq.t[:, :], lhsT=mixT.t[:, k, tt * 512 + s * 128:tt * 512 + (s + 1) * 128],
                                                               rhs=woutb_sb.t[:, k, nh * 512:(nh + 1) * 512], start=(k == ks[0]), stop=(k == ks[-1]))
                                  return i
                              S.op(S.pe, f, reads=[mixT.b, woutb_sb.b], writes=[pA.b, pB.b, pC.b])
                              xs_ = xt.t[:, s, nh * 512:(nh + 1) * 512]
                              S.op(S.dve, lambda: nc.vector.scalar_tensor_tensor(out=xs_, in0=pA.t[:, :], scalar=rsm.t[:, s:s + 1], in1=xs_,
                                                                                 op0=ALU.mult, op1=ALU.add),
                                   reads=[pA.b, rsm.b, xt.b], writes=[xt.b], par=True)
                              S.op(S.dve, lambda: nc.vector.tensor_tensor(out=xs_, in0=pB.t[:, :], in1=xs_, op=ALU.add),
                                   reads=[pB.b, xt.b], writes=[xt.b], par=True)
                              S.op(S.dve, lambda: nc.vector.scalar_tensor_tensor(out=xs_, in0=pC.t[:, :], scalar=rsm.t[:, 4 + s:5 + s], in1=xs_,
                                                                                 op0=ALU.mult, op1=ALU.add),
                                   reads=[pC.b, rsm.b, xt.b], writes=[xt.b], par=True)
                      for s in range(4):
                          S.op(S.act, lambda: nc.scalar.activation(out=junk.t[:, :], in_=xt.t[:, s, :], func=AF.Square,
                                                                   accum_out=ssq.t[:, s:s + 1]),
                               reads=[xt.b], writes=[junk.b, ssq.b])
                      rstd_from(ssq.t[:, :], rs.t[:, :], D_, [ssq.b], rs.b)
                      for s in range(4):
                          S.op(S.dve, lambda: nc.vector.tensor_scalar(out=xn3.t[:, s, :], in0=xt.t[:, s, :],
                                                                      scalar1=rs.t[:, s:s + 1], scalar2=None, op0=ALU.mult),
                               reads=[rs.b, xt.b], writes=[xn3.b], par=(s > 0))
                  def X_b(tt):
                      nonlocal wi_idx
                      tsl_ = slice(tt * 512, (tt + 1) * 512)
                      xt = xt3s[tt % 2]
                      for c in range(8):
                          pt = pb[6 + c % 2]

                          def f():
                              for s in range(4):
                                  i = nc.tensor.transpose(out=pt.t[:, s * 128:(s + 1) * 128], in_=xn3.t[:, s, c * 128:(c + 1) * 128],
                                                          identity=identf.t[:, :])
                              return i
                          S.op(S.pe, f, reads=[xn3.b, identf.b], writes=[pt.b])
                          S.op(S.act, lambda: nc.scalar.activation(out=hT3s[tt % 2].t[:, c, :], in_=pt.t[:, :], func=AF.Identity,
                                                                   bias=modcol.t[:, 24 + c:25 + c], scale=A2.t[:, c:c + 1]),
                               reads=[pt.b, modcol.b, A2.b], writes=[hT3s[tt % 2].b], par=True)
                  def Y_a(tt):
                      nonlocal wi_idx
                      tsl_ = slice(tt * 512, (tt + 1) * 512)
                      xt = xt3s[tt % 2]
                      for j in range(NJ):
                          wneed(wi_idx)
                          wi = wis[wi_idx % 3]
                          wi_idx += 1
                          pg, pu = pb[4 + (j % 2) * 2], pb[5 + (j % 2) * 2]

                          def f():
                              for gi_, pq in enumerate((pg, pu)):
                                  for k in range(8):
                                      i = nc.tensor.matmul(pq.t[:, :], lhsT=wi.t[:, gi_, k, :], rhs=hT3s[tt % 2].t[:, k, :], start=(k == 0), stop=(k == 7))
                              return i
                          S.op(S.pe, f, reads=[wi.b, hT3s[tt % 2].b], writes=[pg.b, pu.b])
                          sl = sil[j % 2]
                          S.op(S.act, lambda: nc.scalar.activation(out=sl.t[:, :], in_=pg.t[:, :], func=AF.Silu), reads=[pg.b], writes=[sl.b])
                          S.op(S.dve, lambda: nc.vector.tensor_tensor(out=actb.t[:, j, :], in0=pu.t[:, :], in1=sl.t[:, :], op=ALU.mult),
                               reads=[pu.b, sl.b], writes=[actb.b], par=True)
                  def Y_b(tt):
                      nonlocal wi_idx
                      tsl_ = slice(tt * 512, (tt + 1) * 512)
                      xt = xt3s[tt % 2]
                      for j in range(NJ):
                          wneed(wi_idx)
                          wo = wos[wi_idx % 3]
                          wi_idx += 1

                          def f():
                              for s in range(4):
                                  for nh in range(2):
                                      i = nc.tensor.matmul(pb[s * 2 + nh].t[:, :], lhsT=actb.t[:, j, s * 128:(s + 1) * 128],
                                                           rhs=wo.t[:, nh * 512:(nh + 1) * 512], start=(j == 0), stop=(j == NJ - 1))
                              return i
                          S.op(S.pe, f, reads=[wo.b, actb.b], writes=[q_.b for q_ in pb])
                      for s in range(4):
                          for nh in range(2):
                              S.op(S.dve, lambda: nc.vector.tensor_tensor(out=xt.t[:, s, nh * 512:(nh + 1) * 512], in0=pb[s * 2 + nh].t[:, :],
                                                                          in1=xt.t[:, s, nh * 512:(nh + 1) * 512], op=ALU.add),
                                   reads=[pb[s * 2 + nh].b, xt.b], writes=[xt.b], par=True)
                      dst = x_dst[tsl_, :].rearrange("(s p) d -> p s d", p=128)
                      if l == 0:
                          S.dma(dst, xt.t[:, :, :], reads=[xt.b], writes=[xdb], par=True, own=xt.b, E=S.act)
                      else:
                          for s in range(4):
                              S.op(S.act, lambda: nc.scalar.activation(out=junk.t[:, :], in_=xt.t[:, s, :], func=AF.Square,
                                                                       accum_out=ssq.t[:, s:s + 1]),
                                   reads=[xt.b], writes=[junk.b, ssq.b])
                          rstd_from(ssq.t[:, :], rs.t[:, :], D_, [ssq.b], rs.b)
                          for s in range(4):
                              S.op(S.dve, lambda: nc.vector.scalar_tensor_tensor(out=xt.t[:, s, :], in0=xt.t[:, s, :], scalar=rs.t[:, s:s + 1],
                                                                                 in1=nfb.t[:, :], op0=ALU.mult, op1=ALU.mult),
                                   reads=[rs.b, xt.b, nfb.b], writes=[xt.b], par=True)
                          S.dma(dst, xt.t[:, :, :], reads=[xt.b], writes=[xdb], par=True, own=xt.b, E=S.act)
                  X_a(0)
                  X_b(0)
                  for tt in range(8):
                      if tt + 1 < 8:
                          xload3(tt + 1)
                      Y_a(tt)
                      if tt + 1 < 8:
                          X_a(tt + 1)
                      Y_b(tt)
                      if tt + 1 < 8:
                          X_b(tt + 1)

                  S.barrier()
                  ck('p3_%d' % l)
        except _Stop:
            pass
        S.finish()
    return nc


_CONST = None


def _prep_inputs(x, c, w_ada, b_ada, norm_mix, w_in, norm_a_out, norm_c_out, w_pool, pool_scale, rpb, w_out,
                 norm_ffn, w_ffn_in, w_ffn_out, norm_final):
    global _CONST
    if _CONST is None:
        _CONST = host_constants()
    ba, cmask, inv = _CONST
    f = lambda a: np.ascontiguousarray(np.asarray(a, dtype=np.float32))
    col = lambda v: np.asarray(v, np.float32).reshape(-1, 128).T
    kc = np.arange(64)[:, None]
    cc = np.arange(64)[None, :]
    idx = np.clip(kc - cc + 15, 0, 30)
    rpbt = f(np.asarray(rpb, np.float32)[:, :, :, idx])
    nfb = f(np.broadcast_to(np.asarray(norm_final, np.float32)[None, :], (128, D_)))
    shared = dict(nfb=nfb, w_ada=f(w_ada), w_in=f(w_in), w_pool=f(w_pool), rpbt=rpbt, w_out=f(w_out),
                  w_ffn_in=f(w_ffn_in), w_ffn_out=f(w_ffn_out), ba_tab=ba, cmask=cmask, invcnt=inv)
    in_maps = []
    for b in range(8):
        parts = [col(c[b])]
        for l in range(2):
            gains = np.concatenate([np.asarray(norm_a_out[l]), np.asarray(pool_scale[l]), np.asarray(norm_c_out[l])])
            parts += [col(b_ada[l]), col(norm_mix[l]), col(norm_ffn[l]), col(gains)]
        m = dict(shared)
        m["x"] = f(x[b])
        m["colpack"] = f(np.concatenate(parts, axis=1))
        in_maps.append(m)
    return in_maps


def kernel(**inputs):
    in_maps = _prep_inputs(**inputs)
    nc = build_nc()
    res = run_bass_kernel_spmd(nc, in_maps, core_ids=list(range(8)))
    return np.stack([np.asarray(res.results[b]["y"], dtype=np.float32) for b in range(8)], axis=0)
```
